# Optimizing a Trainium2 kernel written in Bass

```python
import math
import jax, jax.numpy as jnp
from jax import lax
import numpy as np

D_MODEL = 1024
BATCH = 8
SEQ = 2048
DEPTH = 1
DEC_BATCH = 128
DEC_SEQ = 4
PAST_LEN = 16384
PAGE_SIZE = 128

GM_WIDTH = D_MODEL
GM_HEADS = 8
GM_HEAD_DIM = GM_WIDTH // GM_HEADS
GM_CHUNK = 128
SSM_WIDTH = D_MODEL
SSM_HEAD_DIM = 64
SSM_HEADS = SSM_WIDTH // SSM_HEAD_DIM
SSM_GROUPS = 2
SSM_STATE = 128
SSM_CHUNK = 128
CONV_WIDTH = 4
CONV_DIM = SSM_WIDTH + 2 * SSM_GROUPS * SSM_STATE
MEM_LEN = 256
MEM_HEADS = 4
MEM_WIDTH = D_MODEL
MEM_HEAD_DIM = MEM_WIDTH // MEM_HEADS

MIX_WIDTH = GM_WIDTH + SSM_WIDTH + MEM_WIDTH
SPLIT_SIZES = (2 * GM_WIDTH, GM_WIDTH, SSM_WIDTH, CONV_DIM, SSM_HEADS, MEM_WIDTH, MEM_WIDTH)
IN_WIDTH = sum(SPLIT_SIZES)
SPLIT_POINTS = tuple(int(v) for v in np.cumsum(SPLIT_SIZES)[:-1])
EPS = 1e-6

kernel_name = 'hymba_gmlp_ssd_memory_step'


def rmsnorm(x, g):
    xf = x.astype(jnp.float32)
    y = xf * lax.rsqrt(jnp.mean(xf * xf, axis=-1, keepdims=True) + EPS)
    return (y * g.astype(jnp.float32)).astype(x.dtype)


def layernorm(x, g, b):
    xf = x.astype(jnp.float32)
    mu = jnp.mean(xf, axis=-1, keepdims=True)
    xc = xf - mu
    var = jnp.mean(xc * xc, axis=-1, keepdims=True)
    y = xc * lax.rsqrt(var + EPS) * g.astype(jnp.float32) + b.astype(jnp.float32)
    return y.astype(x.dtype)


def gmlp_branch(uv, gm_norm_g, gm_norm_b, w_s, b_s):
    bsz, t, _ = uv.shape
    lc = min(t, GM_CHUNK)
    uv = jax.nn.gelu(uv, approximate=False)
    u, v = jnp.split(uv, 2, axis=-1)
    v = layernorm(v, gm_norm_g, gm_norm_b)
    causal = jnp.tril(jnp.ones((lc, lc), dtype=bool))
    w = jnp.where(causal, w_s[:, :lc, :lc], 0)
    vc = v.reshape(bsz, t // lc, lc, GM_HEADS, GM_HEAD_DIM)
    mixed = jnp.einsum('hts,bcshd->bcthd', w, vc) + b_s[:, :lc].T[None, None, :, :, None]
    out = u * mixed.reshape(bsz, t, GM_WIDTH)
    return out, v


def causal_conv(xbc, conv_state, conv_w, conv_b):
    xp = jnp.concatenate([conv_state.astype(xbc.dtype), xbc], axis=1)
    y = lax.conv_general_dilated(
        xp, conv_w[:, None, :].astype(xbc.dtype), window_strides=(1,), padding='VALID',
        dimension_numbers=('NWC', 'WIO', 'NWC'), feature_group_count=CONV_DIM)
    new_state = xp[:, xp.shape[1] - (CONV_WIDTH - 1):]
    return jax.nn.silu(y + conv_b), new_state


def ssd_scan(x, dt, a, bm, cm, h0):
    f32 = jnp.float32
    bsz, t = x.shape[:2]
    lc = math.gcd(t, SSM_CHUNK)
    nc = t // lc
    hg = SSM_HEADS // SSM_GROUPS
    x = x.astype(f32).reshape(bsz, nc, lc, SSM_GROUPS, hg, SSM_HEAD_DIM)
    dt = dt.reshape(bsz, nc, lc, SSM_GROUPS, hg)
    bm = bm.astype(f32).reshape(bsz, nc, lc, SSM_GROUPS, SSM_STATE)
    cm = cm.astype(f32).reshape(bsz, nc, lc, SSM_GROUPS, SSM_STATE)
    acum = jnp.cumsum(dt * a.reshape(SSM_GROUPS, hg), axis=2)
    diff = acum[:, :, :, None] - acum[:, :, None, :]
    causal = jnp.tril(jnp.ones((lc, lc), dtype=bool))[:, :, None, None]
    decay = jnp.exp(jnp.where(causal, diff, -jnp.inf))
    xdt = x * dt[..., None]
    cb = jnp.einsum('bclgn,bcsgn->bclsg', cm, bm)
    y_diag = jnp.einsum('bclsg,bclsgk,bcsgkp->bclgkp', cb, decay, xdt)
    to_end = jnp.exp(acum[:, :, -1:] - acum)
    chunk_states = jnp.einsum('bclgn,bclgk,bclgkp->bcgkpn', bm, to_end, xdt)
    chunk_decay = jnp.exp(acum[:, :, -1])

    def step(h, inp):
        s_c, d_c = inp
        return h * d_c[..., None, None] + s_c, h

    h_init = h0.astype(f32).reshape(bsz, SSM_GROUPS, hg, SSM_HEAD_DIM, SSM_STATE)
    h_final, h_prev = lax.scan(step, h_init,
                               (jnp.moveaxis(chunk_states, 1, 0), jnp.moveaxis(chunk_decay, 1, 0)))
    h_prev = jnp.moveaxis(h_prev, 0, 1)
    y_off = jnp.einsum('bclgn,bcgkpn,bclgk->bclgkp', cm, h_prev, jnp.exp(acum))
    y = (y_diag + y_off).reshape(bsz, t, SSM_HEADS, SSM_HEAD_DIM)
    return y, h_final.reshape(bsz, SSM_HEADS, SSM_HEAD_DIM, SSM_STATE)


def mamba2_branch(z, xbc, dt_raw, conv_state, ssm_state, conv_w, conv_b, dt_bias, a_log,
                  d_skip, ssm_norm_g):
    f32 = jnp.float32
    bsz, t, _ = z.shape
    xbc, new_conv = causal_conv(xbc, conv_state, conv_w, conv_b)
    xs, bm, cm = jnp.split(xbc, [SSM_WIDTH, SSM_WIDTH + SSM_GROUPS * SSM_STATE], axis=-1)
    xs = xs.reshape(bsz, t, SSM_HEADS, SSM_HEAD_DIM)
    bm = bm.reshape(bsz, t, SSM_GROUPS, SSM_STATE)
    cm = cm.reshape(bsz, t, SSM_GROUPS, SSM_STATE)
    dt = jax.nn.softplus(dt_raw.astype(f32) + dt_bias.astype(f32))
    a = -jnp.exp(a_log.astype(f32))
    y, h_new = ssd_scan(xs, dt, a, bm, cm, ssm_state)
    y = y + d_skip.astype(f32)[:, None] * xs.astype(f32)
    y = y.reshape(bsz, t, SSM_WIDTH) * jax.nn.silu(z.astype(f32))
    yg = y.reshape(bsz, t, SSM_GROUPS, SSM_WIDTH // SSM_GROUPS)
    yg = yg * lax.rsqrt(jnp.mean(yg * yg, axis=-1, keepdims=True) + EPS)
    y = yg.reshape(bsz, t, SSM_WIDTH) * ssm_norm_g.astype(f32)
    return y.astype(z.dtype), h_new.astype(ssm_state.dtype), new_conv


def memory_kv(mem, mem_norm_g, w_mem_k, w_mem_v):
    bsz = mem.shape[0]
    m = rmsnorm(mem, mem_norm_g)
    k = jnp.einsum('bmd,de->bme', m, w_mem_k).reshape(bsz, MEM_LEN, MEM_HEADS, MEM_HEAD_DIM)
    v = jnp.einsum('bmd,de->bme', m, w_mem_v).reshape(bsz, MEM_LEN, MEM_HEADS, MEM_HEAD_DIM)
    return k, v


def memory_branch(q, gate, mem_k, mem_v):
    bsz, t, _ = q.shape
    qh = q.reshape(bsz, t, MEM_HEADS, MEM_HEAD_DIM).astype(jnp.float32)
    s = jnp.einsum('bthd,bmhd->bhtm', qh, mem_k.astype(jnp.float32)) * (MEM_HEAD_DIM ** -0.5)
    p = jax.nn.softmax(s, axis=-1)
    o = jnp.einsum('bhtm,bmhd->bthd', p.astype(mem_v.dtype), mem_v).reshape(bsz, t, MEM_WIDTH)
    return (o * jax.nn.silu(gate)).astype(q.dtype)


def hybrid_layer(x, conv_state, ssm_state, mem_k, mem_v, norm_g, w_in, gm_norm_g, gm_norm_b,
                 gm_w_spatial, gm_b_spatial, conv_w, conv_b, dt_bias, a_log, d_skip,
                 ssm_norm_g, w_out):
    hn = rmsnorm(x, norm_g)
    proj = jnp.einsum('btd,de->bte', hn, w_in)
    gm_uv, gm_gate, ssm_z, ssm_xbc, ssm_dt, mem_q, mem_gate = jnp.split(proj, SPLIT_POINTS, axis=-1)
    a_out, v_rows = gmlp_branch(gm_uv, gm_norm_g, gm_norm_b, gm_w_spatial, gm_b_spatial)
    a_out = a_out * jax.nn.silu(gm_gate)
    b_out, ssm_new, conv_new = mamba2_branch(ssm_z, ssm_xbc, ssm_dt, conv_state, ssm_state,
                                             conv_w, conv_b, dt_bias, a_log, d_skip, ssm_norm_g)
    c_out = memory_branch(mem_q, mem_gate, mem_k, mem_v)
    mixed = jnp.concatenate([a_out, b_out, c_out], axis=-1)
    y = x + jnp.einsum('bte,ed->btd', mixed, w_out)
    return y, ssm_new, conv_new, v_rows


def setup_inputs(seed: int = 0) -> dict:
    key = jax.random.key(seed)
    ks = jax.random.split(key, 26)
    f32 = jnp.float32

    def nrm(k, shape, scale):
        return jax.random.normal(k, shape, f32) * scale

    u_dt = jax.random.uniform(ks[14], (DEPTH, SSM_HEADS), f32)
    dt0 = jnp.exp(u_dt * (math.log(0.1) - math.log(0.001)) + math.log(0.001))
    return {
        'x_prompt': nrm(ks[0], (BATCH, SEQ, D_MODEL), 1.0),
        'x_sample': nrm(ks[1], (DEC_BATCH, DEC_SEQ, D_MODEL), 1.0),
        'mem_prompt': nrm(ks[2], (BATCH, MEM_LEN, D_MODEL), 1.0),
        'state_ssm': nrm(ks[3], (DEPTH, DEC_BATCH, SSM_HEADS, SSM_HEAD_DIM, SSM_STATE), 0.1),
        'state_conv': nrm(ks[4], (DEPTH, DEC_BATCH, CONV_WIDTH - 1, CONV_DIM), 1.0),
        'cache_mem_k': nrm(ks[5], (DEPTH, DEC_BATCH, MEM_LEN, MEM_HEADS, MEM_HEAD_DIM), 1.0),
        'cache_mem_v': nrm(ks[6], (DEPTH, DEC_BATCH, MEM_LEN, MEM_HEADS, MEM_HEAD_DIM), 1.0),
        'norm_g': 1.0 + nrm(ks[7], (DEPTH, D_MODEL), 0.02),
        'w_in': nrm(ks[8], (DEPTH, D_MODEL, IN_WIDTH), D_MODEL ** -0.5),
        'gm_norm_g': 1.0 + nrm(ks[9], (DEPTH, GM_WIDTH), 0.02),
        'gm_norm_b': nrm(ks[10], (DEPTH, GM_WIDTH), 0.02),
        'gm_w_spatial': nrm(ks[11], (DEPTH, GM_HEADS, GM_CHUNK, GM_CHUNK), 0.05),
        'gm_b_spatial': 1.0 + nrm(ks[12], (DEPTH, GM_HEADS, GM_CHUNK), 0.02),
        'conv_w': nrm(ks[13], (DEPTH, CONV_WIDTH, CONV_DIM), CONV_WIDTH ** -0.5),
        'conv_b': nrm(ks[15], (DEPTH, CONV_DIM), 0.02),
        'dt_bias': dt0 + jnp.log(-jnp.expm1(-dt0)),
        'a_log': jnp.log(jax.random.uniform(ks[16], (DEPTH, SSM_HEADS), f32, minval=1.0, maxval=16.0)),
        'd_skip': 1.0 + nrm(ks[17], (DEPTH, SSM_HEADS), 0.02),
        'ssm_norm_g': 1.0 + nrm(ks[18], (DEPTH, SSM_WIDTH), 0.02),
        'mem_norm_g': 1.0 + nrm(ks[19], (DEPTH, D_MODEL), 0.02),
        'w_mem_k': nrm(ks[20], (DEPTH, D_MODEL, MEM_WIDTH), D_MODEL ** -0.5),
        'w_mem_v': nrm(ks[21], (DEPTH, D_MODEL, MEM_WIDTH), D_MODEL ** -0.5),
        'w_out': nrm(ks[22], (DEPTH, MIX_WIDTH, D_MODEL), MIX_WIDTH ** -0.5),
        'final_norm_g': 1.0 + nrm(ks[23], (D_MODEL,), 0.02),
    }


def reference(x_prompt, x_sample, mem_prompt, state_ssm, state_conv, cache_mem_k, cache_mem_v,
              norm_g, w_in, gm_norm_g, gm_norm_b, gm_w_spatial, gm_b_spatial, conv_w, conv_b,
              dt_bias, a_log, d_skip, ssm_norm_g, mem_norm_g, w_mem_k, w_mem_v, w_out,
              final_norm_g):
    bp = x_prompt.shape[0]
    h_p = x_prompt
    h_s = x_sample
    ssm_p, conv_p, mk_p, mv_p = [], [], [], []
    ssm_s, conv_s, gv_s = [], [], []
    for i in range(DEPTH):
        lw = (norm_g[i], w_in[i], gm_norm_g[i], gm_norm_b[i], gm_w_spatial[i], gm_b_spatial[i],
              conv_w[i], conv_b[i], dt_bias[i], a_log[i], d_skip[i], ssm_norm_g[i], w_out[i])
        mk, mv = memory_kv(mem_prompt, mem_norm_g[i], w_mem_k[i], w_mem_v[i])
        conv0 = jnp.zeros((bp, CONV_WIDTH - 1, CONV_DIM), x_prompt.dtype)
        ssm0 = jnp.zeros((bp, SSM_HEADS, SSM_HEAD_DIM, SSM_STATE), state_ssm.dtype)
        h_p, s_new, c_new, _ = hybrid_layer(h_p, conv0, ssm0, mk, mv, *lw)
        ssm_p.append(s_new)
        conv_p.append(c_new)
        mk_p.append(mk)
        mv_p.append(mv)
        h_s, s_new, c_new, v_rows = hybrid_layer(h_s, state_conv[i], state_ssm[i],
                                                 cache_mem_k[i], cache_mem_v[i], *lw)
        ssm_s.append(s_new)
        conv_s.append(c_new)
        gv_s.append(v_rows)
    y_prompt = rmsnorm(h_p, final_norm_g)
    y_sample = rmsnorm(h_s, final_norm_g)
    return (y_prompt, y_sample, jnp.stack(ssm_p), jnp.stack(conv_p), jnp.stack(mk_p),
            jnp.stack(mv_p), jnp.stack(ssm_s), jnp.stack(conv_s), jnp.stack(gv_s))
```

```python
import contextlib
import numpy as np
import concourse.bass as bass
import concourse.mybir as mybir
from concourse.bass_utils import run_bass_kernel_spmd

F32 = mybir.dt.float32
BF16 = mybir.dt.bfloat16
AF = mybir.ActivationFunctionType
ALU = mybir.AluOpType
AX = mybir.AxisListType

NCORES = 8
EPS = 1e-6
NEG = -30000.0


class Buf:
    __slots__ = ("name", "w", "r", "excl")

    def __init__(self, name="", excl=False):
        self.name = name
        self.w = {}
        self.r = {}
        self.excl = excl


class Op:
    __slots__ = ("eng", "fn", "deps", "needed", "count", "dma", "dsem", "dval", "waits", "sw", "cost", "sdeps", "idx", "tset", "tag", "line")

    def __init__(self, eng, fn, dma=False):
        self.eng = eng
        self.fn = fn
        self.deps = []
        self.needed = False
        self.count = None
        self.dma = dma
        self.dsem = None
        self.dval = None
        self.waits = None
        self.sw = False
        self.cost = 0.3
        self.sdeps = []
        self.idx = 0
        self.tset = None
        self.tag = None


class Sched:
    ENGS = ("pe", "dve", "act", "pool", "sp")

    def __init__(self, nc, n_dma_sems=64, n_hw=40):
        self.nc = nc
        self.ops = {e: [] for e in self.ENGS}
        self.n_dma_sems = n_dma_sems
        self.dma_uses = [0] * n_dma_sems
        self.dma_last = [None] * n_dma_sems
        self.dma_rr = 0
        self.pen = 1.3
        self.nops = 0
        self.sw_rr = 0
        self.n_hw = n_hw

    def add(self, eng, fn, reads=(), writes=(), dma=False, cost=0.3):
        op = Op(eng, fn, dma)
        op.cost = cost
        op.idx = self.nops
        self.nops += 1
        op.tag = getattr(self, "cur_tag", None)
        op.line = None
        deps = []
        xr = [b for b in reads if b.excl]
        if xr:
            reads = [b for b in reads if not b.excl]
            writes = list(writes) + [b for b in xr if b not in writes]
        for b in reads:
            for t in b.w.values():
                deps.append((t, "raw"))
        for b in writes:
            for t in b.w.values():
                deps.append((t, "waw"))
            for lst in b.r.values():
                for t in lst:
                    deps.append((t, "war"))
        if dma:
            if eng == "pool":
                s = self.n_hw + self.sw_rr
                self.sw_rr = (self.sw_rr + 1) % (self.n_dma_sems - self.n_hw)
            else:
                s = self.dma_rr
                self.dma_rr = (self.dma_rr + 1) % self.n_hw
            prev = self.dma_last[s]
            if prev is not None:
                deps.append((prev, "dmasem"))
            self.dma_uses[s] += 1
            op.dsem = s
            op.dval = 16 * self.dma_uses[s]
            self.dma_last[s] = op
            op.needed = True
        for t, kind in deps:
            if t is op:
                continue
            if t.dma:
                op.deps.append(t)
            else:
                if t.eng == eng and not dma:
                    if eng == "pe":
                        op.sdeps.append(t)
                        continue
                t.needed = True
                op.deps.append(t)
        key = ("d", op.dsem) if dma else eng
        for b in reads:
            b.r.setdefault(key, []).append(op)
        for b in writes:
            b.w = {key: op}
            b.r = {}
        self.ops[eng].append(op)
        return op

    def dma(self, q, out, in_, reads=(), writes=()):
        n = 1
        for d in out.shape:
            n *= d
        bpe = 2 if (out.dtype == BF16 and in_.dtype == BF16) else 4
        return self.add(q, lambda e: e.dma_start(out=out, in_=in_), reads, writes, dma=True, cost=2.0 + n * bpe / 150e3)

    def schedule(self):
        import heapq
        allops = [op for e in self.ENGS for op in self.ops[e]]
        succ = {id(op): [] for op in allops}
        indeg = {id(op): 0 for op in allops}
        for op in allops:
            for t in list(op.deps) + list(op.sdeps):
                succ[id(t)].append(op)
                indeg[id(op)] += 1
        LAT = 0.25
        blev = {}
        for op in sorted(allops, key=lambda o: -o.idx):
            m = 0.0
            for s_ in succ[id(op)]:
                v = blev[id(s_)] + LAT
                if v > m:
                    m = v
            blev[id(op)] = m + op.cost
        ready_t = {id(op): 0.0 for op in allops}
        wait_h = {e: [] for e in self.ENGS}
        for op in allops:
            if indeg[id(op)] == 0:
                heapq.heappush(wait_h[op.eng], (0.0, op.idx, op))
        free = {e: 0.0 for e in self.ENGS}
        new = {e: [] for e in self.ENGS}
        left = len(allops)
        cur_tset = [None]
        while left:
            best = None
            for e in self.ENGS:
                h = wait_h[e]
                if not h:
                    continue
                t_now = max(free[e], h[0][0])
                cands = [c for c in heapq.nsmallest(24, h) if c[0] <= t_now + (2.0 if e == "act" else 0.05)]
                cb = None
                for rt, idx, op in cands:
                    pen = self.pen if (e == "act" and op.tset is not None and op.tset != cur_tset[0]) else 0.0
                    key = (-(blev[id(op)] - 40.0 * pen), idx)
                    if cb is None or key < cb[0]:
                        cb = (key, rt, idx, op, pen)
                key, rt, idx, op, pen = cb
                st = max(rt, free[e]) + pen
                if best is None or (st, idx) < (best[0], best[1]):
                    best = (st, idx, op, rt)
            st, idx, op, rt0 = best
            h_ = wait_h[op.eng]
            if h_[0][2] is op:
                heapq.heappop(h_)
            else:
                h_.remove((rt0, idx, op))
                heapq.heapify(h_)
            if op.eng == "act" and op.tset is not None:
                cur_tset[0] = op.tset
            if op.dma:
                free[op.eng] = st + (1.0 if op.eng == "pool" else 0.1)
            else:
                free[op.eng] = st + op.cost
            fin = st + op.cost
            new[op.eng].append(op)
            left -= 1
            for s_ in succ[id(op)]:
                k = id(s_)
                if fin + LAT > ready_t[k]:
                    ready_t[k] = fin + LAT
                indeg[k] -= 1
                if indeg[k] == 0:
                    heapq.heappush(wait_h[s_.eng], (ready_t[k], s_.idx, s_))
        self.ops = new
        self.est = max(free.values())

    def finalize(self, final_eng="sp"):
        for e in self.ENGS:
            c = 0
            for op in self.ops[e]:
                if op.dma:
                    continue
                if op.needed:
                    c += 1
                    op.count = c
        tail = Op(final_eng, None)
        for s in range(self.n_dma_sems):
            if self.dma_last[s] is not None:
                tail.deps.append(self.dma_last[s])
        self.ops[final_eng].append(tail)
        for e in self.ENGS:
            seen = {}
            for op in self.ops[e]:
                need = {}
                for t in op.deps:
                    if t.dma:
                        k = ("d", t.dsem)
                        v = t.dval
                    else:
                        k = t.eng
                        v = t.count
                    if v > need.get(k, 0):
                        need[k] = v
                w = []
                for k, v in need.items():
                    if v > seen.get(k, 0):
                        seen[k] = v
                        w.append((k, v))
                op.waits = w

    def emit(self):
        nc = self.nc
        if getattr(self, "do_sched", True):
            self.schedule()
        self.finalize()
        with contextlib.ExitStack() as st:
            esem = {e: st.enter_context(nc.semaphore("s_" + e)) for e in ("pe", "dve", "act", "pool")}
            dsem = [st.enter_context(nc.semaphore("d%d" % i)) for i in range(self.n_dma_sems)]
            block = st.enter_context(nc.Block())

            def run(ename, eng):
                for op in self.ops[ename]:
                    for k, v in op.waits:
                        if isinstance(k, tuple):
                            eng.wait_ge(dsem[k[1]], v)
                        else:
                            eng.wait_ge(esem[k], v)
                    if op.fn is None:
                        continue
                    ins = op.fn(eng)
                    if op.dma:
                        ins.then_inc(dsem[op.dsem], 16)
                    elif op.needed:
                        ins.then_inc(esem[ename], 1)

            @block.tensor
            def _(eng):
                run("pe", eng)

            @block.vector
            def _(eng):
                run("dve", eng)

            @block.scalar
            def _(eng):
                run("act", eng)

            @block.gpsimd
            def _(eng):
                run("pool", eng)

            @block.sync
            def _(eng):
                run("sp", eng)


SPLITS = dict(u=0, v=1024, gate=2048, z=3072, xbc=4096, dt=5632, q=5648, mg=6672)


def build_program(stop=None):
    nc = bass.Bass("TRN2", target_bir_lowering=False)
    S = Sched(nc)
    S.marks = []

    def mark(lbl):
        S.cur_tag = lbl
        S.marks.append((lbl, len(S.ops['dve']), len(S.ops['act']), len(S.ops['pe'])))

    def din(name, shape):
        return nc.dram_tensor(name, list(shape), F32, kind="ExternalInput").ap()

    def dout(name, shape):
        return nc.dram_tensor(name, list(shape), F32, kind="ExternalOutput").ap()

    x_p = din("x_p", [2048, 1024]); x_s = din("x_s", [64, 1024]); mem = din("mem", [256, 1024])
    st_ssm = din("st_ssm", [16, 8, 128, 128]); st_conv = din("st_conv", [48, 1536])
    ck = din("ck", [16, 256, 1024]); cv = din("cv", [16, 256, 1024])
    norm_g = din("norm_g", [1, 1024]); w_in = din("w_in", [1024, 7696])
    gm_norm_g = din("gm_norm_g", [1, 1024]); gm_norm_b = din("gm_norm_b", [1, 1024])
    gm_w = din("gm_w", [8, 128, 128]); gm_bs = din("gm_bs", [8, 128])
    conv_w = din("conv_w", [4, 1536]); conv_b = din("conv_b", [1, 1536])
    dt_bias = din("dt_bias", [1, 16]); a_log = din("a_log", [1, 16]); d_skip = din("d_skip", [1, 16])
    ssm_norm_g = din("ssm_norm_g", [1, 1024]); mem_norm_g = din("mem_norm_g", [1, 1024])
    w_mk = din("w_mk", [1024, 1024]); w_mv = din("w_mv", [1024, 1024]); w_out = din("w_out", [3072, 1024])
    fng = din("fng", [1, 1024])
    c_ident = din("c_ident", [128, 128]); c_utri = din("c_utri", [128, 128]); c_utri_s = din("c_utri_s", [64, 64])
    c_same_s = din("c_same_s", [64, 64]); c_mask_p = din("c_mask_p", [128, 512]); c_mask_s = din("c_mask_s", [64, 256])
    c_d96 = din("c_d96", [96, 512]); c_d96s = din("c_d96s", [96, 256])
    c_oh16 = din("c_oh16", [16, 8]); c_msel = din("c_msel", [16, 2])
    c_seqm = din("c_seqm", [1, 1024]); c_seqm_t = din("c_seqm_t", [64, 16]); c_sel4 = din("c_sel4", [4, 64])

    y_p = dout("y_p", [2048, 1024]); y_s = dout("y_s", [64, 1024])
    o_ssm_p = dout("o_ssm_p", [8, 128, 128]); o_conv_p = dout("o_conv_p", [3, 1536])
    o_mk = dout("o_mk", [256, 1024]); o_mv = dout("o_mv", [256, 1024])
    o_ssm_s = dout("o_ssm_s", [16, 8, 128, 128]); o_conv_s = dout("o_conv_s", [48, 1536])
    o_gv = dout("o_gv", [64, 1024])

    def sb(name, shape, dt=F32):
        return nc.alloc_sbuf_tensor(name, list(shape), dt), Buf(name)

    NTMAX = 576
    hnT, hnT_B0 = sb("hnT", [128, 8, NTMAX], BF16)
    mixT, mixT_B0 = sb("mixT", [128, 24, NTMAX], BF16)
    hnTB = [Buf("hnT%d" % i) for i in range(5)]
    mxB = [[Buf("mix%d_%d" % (k, q)) for q in range(2)] for k in range(24)]
    NSLOT = 5
    ring = [sb("ring%d" % i, [128, 4096], BF16) for i in range(NSLOT)]
    NWK = 22
    wk = [sb("wk%d" % i, [128, 1024], F32) for i in range(NWK)]

    ident_f, ident_f_B = sb("ident_f", [128, 128]); ident_b, ident_b_B = sb("ident_b", [128, 128], BF16)
    utri, utri_B = sb("utri", [128, 128]); utri_s, utri_s_B = sb("utri_s", [64, 64])
    ones_f, ones_f_B = sb("ones_f", [128, 128]); ones_b, ones_b_B = sb("ones_b", [128, 128], BF16)
    same_s, same_s_B = sb("same_s", [64, 64])
    mask_p, mask_p_B = sb("mask_p", [128, 512], BF16); mask_s, mask_s_B = sb("mask_s", [64, 256], BF16)
    d96, d96_B = sb("d96", [96, 512], BF16); d96s, d96s_B = sb("d96s", [96, 256], BF16)
    nones_b, nones_b_B = sb("nones_b", [128, 128], BF16)
    dtAz = [sb("dtAz%d" % i, [128, 384]) for i in range(2)]
    hTb_bufs = [Buf("hTb0"), Buf("hTb1")]
    yb_bufs = [Buf("yb0"), Buf("yb1")]
    r1h_bufs = [[Buf("r1h%d%d" % (a_, b_)) for b_ in range(2)] for a_ in range(2)]
    seqm, seqm_B = sb("seqm", [128, 16, 64], BF16)
    seqm_t, seqm_t_B = sb("seqm_t", [64, 16])
    sel4, sel4_B = sb("sel4", [4, 64])
    bc = [sb("bc%d" % i, [128, 1024]) for i in range(2)]
    hvec, hvec_B = sb("hvec", [128, 4, 16])
    dsk_b, dsk_b_B = sb("dsk_b", [128, 16], BF16)
    eps_c, eps_c_B = sb("eps_c", [128, 1])
    bs16, bs16_B = sb("bs16", [16, 128], BF16); bs16s, bs16s_B = sb("bs16s", [16, 64], BF16)
    oh16, oh16_B = sb("oh16", [16, 8]); msel, msel_B = sb("msel", [16, 2])
    WT, WT_B = sb("WT", [128, 8, 128], BF16); BD, BD_B = sb("BD", [64, 8, 64], BF16)
    wdt, wdt_B = sb("wdt", [128, 8, 16], BF16)
    kT, kT_B = sb("kT", [128, 8, 256], BF16); v_tm, v_tm_B = sb("v_tm", [128, 2, 1024], BF16)
    hT, hT_B = sb("hT", [128, 1024])
    convhist, convhist_B = sb("convhist", [128, 12, 3])
    scT, scT_B = sb("scT", [128, 12, 48])
    cwc, cwc_B = sb("cwc", [128, 12, 8])
    sm = [sb("sm%d" % i, [128, 64]) for i in range(14)]
    qTs, qTs_B = sb("qTs", [128, 8, 64], BF16); mgss, mgss_B = sb("mgss", [128, 8, 64], BF16)

    pp = [nc.alloc_psum_tensor("pp%d" % i, [128, 1024], F32) for i in range(4)]
    ppb = [t[:].bitcast(BF16) for t in pp]
    PB = [Buf("pb%d" % i, excl=True) for i in range(8)]
    bank_ctr = [0]
    pinned = set()

    def _next1():
        while True:
            i = bank_ctr[0] % 8
            bank_ctr[0] += 1
            if i not in pinned:
                return i

    def _next2():
        while True:
            if bank_ctr[0] % 2:
                bank_ctr[0] += 1
            i = bank_ctr[0] % 8
            bank_ctr[0] += 2
            if i not in pinned and (i + 1) not in pinned:
                return i

    def bank():
        i = _next1()
        return pp[i // 2][:, (i % 2) * 512:(i % 2) * 512 + 512], PB[i]

    def bank_bf():
        i = _next1()
        return ppb[i // 2][:, (i % 2) * 1024:(i % 2) * 1024 + 1024], PB[i]

    def bank2():
        i = _next2()
        return pp[i // 2], [PB[i], PB[i + 1]]

    def bank2_bf():
        i = _next2()
        return ppb[i // 2], [PB[i], PB[i + 1]]

    def pin(bufs):
        for b in (bufs if isinstance(bufs, list) else [bufs]):
            pinned.add(PB.index(b))

    def unpin(bufs):
        for b in (bufs if isinstance(bufs, list) else [bufs]):
            pinned.discard(PB.index(b))

    def fsz(ap):
        n = 1
        for d in ap.shape[1:]:
            n *= d
        return n

    def mm(out, lhsT, rhs, start, stop, r, w, skip=False):
        c = max(fsz(out), 64) / 2400.0 * (4 if lhsT.dtype == F32 else 1) + 0.02
        S.add("pe", lambda e: e.matmul(out, lhsT, rhs, start=start, stop=stop, skip_group_check=skip), r, w, cost=c)

    def tr(out, in_, ident, r, w):
        S.add("pe", lambda e: e.transpose(out, in_, ident), r, w, cost=0.11)

    def act(out, in_, func, r, w, bias=None, scale=None):
        kw = {}
        if bias is not None:
            kw["bias"] = bias
        if scale is not None:
            kw["scale"] = scale
        o_ = S.add("act", lambda e: e.activation(out, in_, func, **kw), r, w, cost=0.25 + fsz(out) / 1200.0)
        o_.tset = {AF.Exp: "EL", AF.Ln: "EL", AF.Silu: "S", AF.Gelu: "G", AF.Sqrt: "Q"}.get(func)

    def ecost(eng, out):
        return 0.15 + fsz(out) / (960.0 if eng == "dve" else 480.0)

    def tt(eng, out, in0, in1, op, r, w):
        S.add(eng, lambda e: e.tensor_tensor(out, in0, in1, op), r, w, cost=ecost(eng, out))

    def ts(eng, out, in0, s1, s2, op0, op1, r, w):
        S.add(eng, lambda e: e.tensor_scalar(out, in0, s1, s2, op0, op1), r, w, cost=ecost(eng, out))

    def ts1(eng, out, in0, s1, op0, r, w):
        S.add(eng, lambda e: e.tensor_single_scalar(out, in0, s1, op0), r, w, cost=ecost(eng, out))

    def stt(eng, out, in0, sc, in1, op0, op1, r, w):
        S.add(eng, lambda e: e.scalar_tensor_tensor(out, in0, sc, in1, op0, op1), r, w, cost=ecost(eng, out))

    def cp(eng, out, in_, r, w):
        if eng == "act":
            S.add("act", lambda e: e.copy(out, in_), r, w, cost=0.25 + fsz(out) / 1200.0)
        else:
            S.add(eng, lambda e: e.tensor_copy(out, in_), r, w, cost=ecost(eng, out))

    def memset(eng, ap, val, w):
        S.add(eng, lambda e: e.memset(ap, val), (), w)

    fscr, fscr_B = sb("fscr", [128, 8])

    def fence(bufs):
        S.add("dve", lambda e: e.memset(fscr[0:1, 0:1], 0.0), (), [fscr_B] + list(bufs), cost=0.1)

    def bf(t):
        return t[:].bitcast(BF16)

    ring_ctr = [0]

    wcache = {}

    def load_w(src_ap, kchunks, ncols, key=None):
        t, b = ring[ring_ctr[0] % NSLOT]
        ring_ctr[0] += 1
        view = t[:, 0:kchunks * ncols].rearrange("p (k c) -> p k c", k=kchunks)
        if key is not None and key in wcache:
            dt_, dB_ = wcache[key]
            S.dma("sp", t[:, 0:kchunks * ncols], dt_, reads=[dB_], writes=[b])
            return view, b
        S.dma("pool", view, src_ap.rearrange("(k p) c -> p k c", p=128), writes=[b])
        if key is not None:
            dt_ = nc.dram_tensor("wc_%s" % "_".join(str(x) for x in key), [128, kchunks * ncols], BF16, kind="Internal").ap()
            dB_ = Buf("wc")
            S.dma("sp", dt_, t[:, 0:kchunks * ncols], reads=[b], writes=[dB_])
            wcache[key] = (dt_, dB_)
        return view, b

    def w_in_block(col0, ncols=512):
        return load_w(w_in[:, col0:col0 + ncols], 8, ncols, key=("in", col0))

    bc_ctr = [0]

    def load_bc(vec):
        t, b = bc[bc_ctr[0] % 2]
        bc_ctr[0] += 1
        S.dma("sp", t[:, :], vec.partition_broadcast(128), writes=[b])
        return t, b

    def mean_var(x_ap, L, width, xB, st, stB, col):
        nchunk = width // 512
        for c in range(nchunk):
            S.add("dve", (lambda c=c: (lambda e: e.bn_stats(st[:L, 32 + 6 * c:38 + 6 * c], x_ap[:, c * 512:(c + 1) * 512])))(), [xB], [stB])
        S.add("dve", lambda e: e.bn_aggr(st[:L, col:col + 2], st[:L, 32:32 + 6 * nchunk].rearrange("p (c s) -> p c s", s=6)), [stB], [stB])

    def rstd_from(st, stB, L, src_col, dst_col, use_mean_col=None, extra_reads=()):
        if use_mean_col is not None:
            stt("dve", st[:L, src_col:src_col + 1], st[:L, use_mean_col:use_mean_col + 1], st[:L, use_mean_col:use_mean_col + 1],
                st[:L, src_col:src_col + 1], ALU.mult, ALU.add, [stB], [stB])
        act(st[:L, dst_col:dst_col + 1], st[:L, src_col:src_col + 1], AF.Ln, [stB, eps_c_B] + list(extra_reads), [stB], bias=eps_c[:L, :], scale=1.0)
        act(st[:L, dst_col:dst_col + 1], st[:L, dst_col:dst_col + 1], AF.Exp, [stB], [stB], scale=-0.5)

    S.dma("sp", ident_f[:, :], c_ident, writes=[ident_f_B])
    S.dma("pool", ident_b[:, :], c_ident, writes=[ident_b_B])
    S.dma("sp", utri[:, :], c_utri, writes=[utri_B])
    S.dma("sp", utri_s[:, :], c_utri_s, writes=[utri_s_B])
    S.dma("sp", same_s[:, :], c_same_s, writes=[same_s_B])
    S.dma("pool", mask_p[:, :], c_mask_p, writes=[mask_p_B])
    S.dma("pool", mask_s[:, :], c_mask_s, writes=[mask_s_B])
    S.dma("pool", d96[:, :], c_d96, writes=[d96_B])
    S.dma("pool", d96s[:, :], c_d96s, writes=[d96s_B])
    memset("dve", nones_b[:, :], -1.0, [nones_b_B])
    for dz_, dzB_ in dtAz:
        memset("dve", dz_[:, :], 0.0, [dzB_])
    S.dma("pool", seqm[:].rearrange("p a b -> p (a b)"), c_seqm.partition_broadcast(128), writes=[seqm_B])
    S.dma("sp", seqm_t[:, :], c_seqm_t, writes=[seqm_t_B])
    S.dma("sp", sel4[:, :], c_sel4, writes=[sel4_B])
    S.dma("sp", oh16[:, :], c_oh16, writes=[oh16_B])
    S.dma("sp", msel[:, :], c_msel, writes=[msel_B])
    bst_, bst_B = wk[3]
    S.dma("sp", bst_[0:8, 0:128], gm_bs, writes=[bst_B])
    S.dma("sp", bst_[8:16, 0:128], gm_bs, writes=[bst_B])
    S.dma("pool", wdt[:], w_in[:, SPLITS["dt"]:SPLITS["dt"] + 16].rearrange("(k p) c -> p k c", p=128), writes=[wdt_B])
    memset("dve", ones_f[:, :], 1.0, [ones_f_B])
    memset("dve", ones_b[:, :], 1.0, [ones_b_B])
    memset("dve", eps_c[:, :], EPS, [eps_c_B])
    memset("dve", hT[:, :], 0.0, [hT_B])
    memset("dve", convhist[:], 0.0, [convhist_B])
    S.dma("sp", hvec[:, 0, :], dt_bias.partition_broadcast(128), writes=[hvec_B])
    S.dma("sp", hvec[:, 1, :], a_log.partition_broadcast(128), writes=[hvec_B])
    S.dma("sp", hvec[:, 2, :], d_skip.partition_broadcast(128), writes=[hvec_B])
    act(hvec[:, 1, :], hvec[:, 1, :], AF.Exp, [hvec_B], [hvec_B])
    ts1("dve", hvec[:, 1, :], hvec[:, 1, :], -1.0, ALU.mult, [hvec_B], [hvec_B])
    cp("dve", dsk_b[:, :], hvec[:, 2, :], [hvec_B], [dsk_b_B])
    bhi_ = bf(bst_)[0:16, 512:640]
    cp("dve", bhi_, bst_[0:16, 0:128], [bst_B], [bst_B])
    tt("dve", bst_[0:16, 128:256], bst_[0:16, 0:128], bhi_, ALU.subtract, [bst_B], [bst_B])
    ts1("dve", bst_[0:16, 128:256], bst_[0:16, 128:256], msel[:, 1:2], ALU.mult, [bst_B, msel_B], [bst_B])
    stt("dve", bs16[:, :], bhi_, msel[:, 0:1], bst_[0:16, 128:256], ALU.mult, ALU.add, [bst_B, msel_B], [bs16_B])
    cp("dve", bs16s[:].rearrange("h (i a) -> h i a", a=4), bs16[:, 0:4].unsqueeze(1).to_broadcast([16, 16, 4]), [bs16_B], [bs16s_B])

    cst = ring[0][0][:].bitcast(F32)
    cstB = ring[0][1]
    S.dma("sp", cst[0:1, 0:1536], conv_b, writes=[cstB])
    S.dma("sp", cst[1:5, 0:1536], conv_w, writes=[cstB])
    pv, pB = bank()
    for c in range(12):
        tr(pv[:, c * 8:c * 8 + 5], cst[0:5, c * 128:(c + 1) * 128], ident_f[0:5, 0:5], [cstB, ident_f_B], [pB])
    cp("dve", cwc[:, :, 0:5], pv[:, 0:96].rearrange("p (c k) -> p c k", k=8)[:, :, 0:5], [pB], [cwc_B])
    sst = ring[1][0][:].bitcast(F32)
    sstB = ring[1][1]
    S.dma("sp", sst[0:48, 0:1536], st_conv, writes=[sstB])
    p2, p2B = bank2()
    for c in range(12):
        tr(p2[:, c * 64:c * 64 + 48], sst[0:48, c * 128:(c + 1) * 128], ident_f[0:48, 0:48], [sstB, ident_f_B], [p2B[c // 8]])
    cp("dve", scT[:, 0:8, :], p2[:, 0:512].rearrange("p (c k) -> p c k", k=64)[:, :, 0:48], [p2B[0]], [scT_B])
    cp("dve", scT[:, 8:12, :], p2[:, 512:768].rearrange("p (c k) -> p c k", k=64)[:, :, 0:48], [p2B[1]], [scT_B])

    g0, g0B = wk[0]
    g1, g1B = wk[1]
    g2, g2B = wk[2]
    S.dma("sp", g0[:].rearrange("p (h s) -> p h s", h=8), gm_w.rearrange("h t s -> t h s"), writes=[g0B])
    p2, p2B = bank2()
    for h in range(8):
        tr(p2[:, h * 128:(h + 1) * 128], g0[:, h * 128:(h + 1) * 128], ident_f[:, :], [g0B, ident_f_B], [p2B[h // 4]])
    for hb in range(2):
        tt("dve", g1[:, hb * 512:(hb + 1) * 512].rearrange("p (h t) -> p h t", h=4),
           p2[:, hb * 512:(hb + 1) * 512].rearrange("p (h t) -> p h t", h=4),
           utri[:, :].unsqueeze(1).to_broadcast([128, 4, 128]), ALU.mult, [p2B[hb], utri_B], [g1B])
    cp("dve", WT[:].rearrange("p h t -> p (h t)"), g1[:, :], [g1B], [WT_B])
    cp("dve", g2[0:4, 0:512].rearrange("p (h j b) -> p h j b", h=8, j=16),
       g1[0:4, :].rearrange("p (h t) -> p h t", h=8)[:, :, 0:4].unsqueeze(2).to_broadcast([4, 8, 16, 4]), [g1B], [g2B])
    pv, pB = bank()
    mm(pv[0:64, 0:512], sel4[0:4, 0:64], g2[0:4, 0:512], True, True, [sel4_B, g2B], [pB])
    tt("dve", BD[:], pv[0:64, 0:512].rearrange("p (h t) -> p h t", h=8),
       same_s[:, :].unsqueeze(1).to_broadcast([64, 8, 64]), ALU.mult, [pB, same_s_B], [BD_B])

    if stop == "setup":
        return nc, S
    def token_norm_to_T(x_tile, xB, L, gbc, gbcB, xs_bf, xsB, st, stB, dstT, dstB, c0):
        mean_var(x_tile[:L, :], L, 1024, xB, st, stB, 0)
        rstd_from(st, stB, L, 1, 2, use_mean_col=0)
        stt("dve", xs_bf[:L, 0:1024], x_tile[:L, :], st[:L, 2:3], gbc[:L, :], ALU.mult, ALU.mult, [xB, stB, gbcB], [xsB])
        pvb, pB_ = bank_bf()
        for k in range(8):
            tr(pvb[:, k * L:(k + 1) * L], xs_bf[:L, k * 128:(k + 1) * 128], ident_b[:L, :L], [xsB, ident_b_B], [pB_])
        cp("act", dstT[:, :, c0:c0 + L], pvb[:, 0:8 * L].rearrange("p (k l) -> p k l", k=8), [pB_], [dstB])

    def phase_M():
        mg_bc, mg_bcB = load_bc(mem_norm_g)
        memnT = bf(wk[3][0])[:, 0:2048].rearrange("p (k m) -> p k m", k=8)
        memnT_B = wk[3][1]
        for i in range(2):
            xt, xB = wk[i]
            S.dma("sp", xt[:, :], mem[i * 128:(i + 1) * 128, :], writes=[xB])
            token_norm_to_T(xt, xB, 128, mg_bc, mg_bcB, bf(wk[2][0])[:, i * 1024:(i + 1) * 1024], wk[2][1], sm[0][0], sm[0][1],
                            memnT, memnT_B, i * 128)
        for which, wsrc, odst in ((0, w_mk, o_mk), (1, w_mv, o_mv)):
            for half in range(2):
                slot, slotB = load_w(wsrc[:, half * 512:(half + 1) * 512], 8, 512)
                for mt in range(2):
                    stg, stgB = wk[4 + 2 * which + mt]
                    pv, pB = bank()
                    for k in range(8):
                        mm(pv[:, 0:512], memnT[:, k, mt * 128:(mt + 1) * 128], slot[:, k, :], k == 0, k == 7, [memnT_B, slotB], [pB])
                    cp("act", stg[:, half * 512:(half + 1) * 512], pv[:, 0:512], [pB], [stgB])
                    if which == 1:
                        cp("dve", v_tm[:, mt, half * 512:(half + 1) * 512], pv[:, 0:512], [pB], [v_tm_B])
                if which == 0:
                    for jp in range(2):
                        pv, pB = bank()
                        for j2 in range(2):
                            jj = jp * 2 + j2
                            for k in range(8):
                                mm(pv[:, j2 * 256:(j2 + 1) * 256], slot[:, k, jj * 128:(jj + 1) * 128], memnT[:, k, :], k == 0, k == 7,
                                   [memnT_B, slotB], [pB])
                        cp("act", kT[:, half * 4 + jp * 2:half * 4 + jp * 2 + 2, :], pv[:, 0:512].rearrange("p (j m) -> p j m", j=2), [pB], [kT_B])
            for mt in range(2):
                stg, stgB = wk[4 + 2 * which + mt]
                S.dma("sp", odst[mt * 128:(mt + 1) * 128, :], stg[:, :], reads=[stgB])


    if stop == "M":
        return nc, S
    NPASS = 4
    PS_A = 1
    PS_C = 1
    PS_B = 2
    PS_D = 2
    for p in range(NPASS):
        tiles_pr = [(t * 128, 128, False) for t in range(4)]
        tiles = tiles_pr + ([(512, 64, True)] if p == PS_A else [])
        tilesB = tiles_pr + ([(512, 64, True)] if p == PS_B else [])
        tilesD = tiles_pr + ([(512, 64, True)] if p == PS_D else [])
        NT = 576 if p == PS_A else 512

        mark("P0%d" % p)
        ng_bc, ng_bcB = load_bc(norm_g)
        for i, (c0, L, smp) in enumerate(tiles):
            xt, xB = wk[i % 2]
            src = x_s[:, :] if smp else x_p[p * 512 + c0:p * 512 + c0 + 128, :]
            S.dma("sp", xt[:L, :], src, writes=[xB])
            token_norm_to_T(xt, xB, L, ng_bc, ng_bcB, bf(wk[2 + i % 2][0]), wk[2 + i % 2][1], sm[(0, 7)[i % 2]][0], sm[(0, 7)[i % 2]][1], hnT, hnTB[i], c0)

        if stop == "P0":
            return nc, S
        mark("A%d" % p)
        gg_bc, gg_bcB = load_bc(gm_norm_g)
        gb_bc, gb_bcB = load_bc(gm_norm_b)
        vb = []
        for i in range(len(tiles)):
            t_, b_ = wk[5 + i // 2]
            vb.append((bf(t_)[:, (i % 2) * 1024:(i % 2) * 1024 + 1024], b_))
        for cb in range(2):
            slot, slotB = w_in_block(SPLITS["v"] + cb * 512)
            for i, (c0, L, smp) in enumerate(tiles):
                pv, pB = bank()
                for k in range(8):
                    mm(pv[:L, 0:512], hnT[:, k, c0:c0 + L], slot[:, k, :], k == 0, k == 7, [hnTB[i], slotB], [pB])
                act(wk[i][0][:L, cb * 512:(cb + 1) * 512], pv[:L, 0:512], AF.Gelu, [pB], [wk[i][1]])
        geluB = Buf("gelu_order")

        def a_layernorm():
            for i, (c0, L, smp) in enumerate(tiles):
                vg, vgB = wk[i]
                st, stB = sm[(0, 7)[i % 2]]
                mean_var(vg[:L, :], L, 1024, vgB, st, stB, 0)
                rstd_from(st, stB, L, 1, 2, extra_reads=[geluB])
                ts("dve", vg[:L, :], vg[:L, :], st[:L, 0:1], st[:L, 2:3], ALU.subtract, ALU.mult, [vgB, stB], [vgB])
                tt("dve", vg[:L, :], vg[:L, :], gg_bc[:L, :], ALU.mult, [vgB, gg_bcB], [vgB])
                if smp:
                    tt("dve", vg[:L, :], vg[:L, :], gb_bc[:L, :], ALU.add, [vgB, gb_bcB], [vgB])
                    S.dma("sp", o_gv, vg[:L, :], reads=[vgB])
                    cp("act", vb[i][0][:L, :], vg[:L, :], [vgB], [vb[i][1]])
                else:
                    tt("dve", vb[i][0][:L, :], vg[:L, :], gb_bc[:L, :], ALU.add, [vgB, gb_bcB], [vb[i][1]])
        bsr, bsrB = wk[10]
        bsrs, bsrsB = wk[13]
        bsrBs = [Buf() for _ in range(8)]
        bsrsBs = [Buf() for _ in range(8)]
        WS_IDS = (8, 9, 11, 12, 14, 15, 16, 17)
        wsbufs = [[Buf() for _ in range(3)] for _ in range(8)]
        a_par = [wk[k_][1] for k_ in WS_IDS] + [wk[10][1], wk[13][1]]
        a_sub = [b_ for l_ in wsbufs for b_ in l_] + bsrBs + bsrsBs
        fence(a_par + a_sub)

        def a_proj(slt, sltB, jj, dst, func, wsB):
            xw_ = [geluB] if func == AF.Gelu else []
            pv, pB = bank()
            for k in range(8):
                mm(pv[:, 0:512], slt[:, k, jj * 128:(jj + 1) * 128], hnT[:, k, 0:512], k == 0, k == 7, hnTB[0:4] + [sltB], [pB])
            act(dst[:, 0:512], pv[:, 0:512], func, [pB], [wsB] + xw_)
            if p == PS_A:
                pv, pB = bank()
                for k in range(8):
                    mm(pv[:, 0:64], slt[:, k, jj * 128:(jj + 1) * 128], hnT[:, k, 512:576], k == 0, k == 7, [hnTB[4], sltB], [pB])
                act(dst[:, 512:576], pv[:, 0:64], func, [pB], [wsB])

        for j in range(8):
            jj = j % 4
            if jj == 0:
                slot_u, slot_uB = w_in_block(SPLITS["u"] + (j // 4) * 512)
            wsb = bf(wk[WS_IDS[j]][0])
            a_proj(slot_u, slot_uB, jj, wsb[:, 0:576], AF.Gelu, wsbufs[j][0])
        a_layernorm()
        for j in range(8):
            jj = j % 4
            if jj == 0:
                slot_g, slot_gB = w_in_block(SPLITS["gate"] + (j // 4) * 512)
            wsb = bf(wk[WS_IDS[j]][0])
            uaB_, sgB_, usgB_ = wsbufs[j]
            ua = wsb[:, 0:576]
            sg = wsb[:, 576:1152]
            usg = wsb[:, 1152:1728]
            a_proj(slot_g, slot_gB, jj, sg, AF.Silu, sgB_)
            tt("dve", usg[:, 0:NT], ua[:, 0:NT], sg[:, 0:NT], ALU.mult, [uaB_, sgB_], [usgB_])
            bsr16 = bf(bsr)[0:16, j * 128:(j + 1) * 128]
            ts1("dve", bsr16, bs16[:, :], oh16[:, j:j + 1], ALU.mult, [bs16_B, oh16_B], [bsrBs[j]])
            pv, pB = bank()
            for i in range(4):
                mm(pv[:, i * 128:(i + 1) * 128], vb[i][0][:, j * 128:(j + 1) * 128], WT[:, j, :], True, False, [vb[i][1], WT_B], [pB])
                mm(pv[:, i * 128:(i + 1) * 128], ones_b[0:16, 0:128], bsr16, False, True, [ones_b_B, bsrBs[j]], [pB])
            tt("dve", mixT[:, j, 0:512], pv[:, 0:512], usg[:, 0:512], ALU.mult, [pB, usgB_], [mxB[j][0]])
            if p == PS_A:
                bsrs16 = bf(bsrs)[0:16, j * 64:(j + 1) * 64]
                ts1("dve", bsrs16, bs16s[:, :], oh16[:, j:j + 1], ALU.mult, [bs16s_B, oh16_B], [bsrsBs[j]])
                pv, pB = bank()
                mm(pv[:, 0:64], vb[4][0][0:64, j * 128:(j + 1) * 128], BD[:, j, :], True, False, [vb[4][1], BD_B], [pB])
                mm(pv[:, 0:64], ones_b[0:16, 0:128], bsrs16, False, True, [ones_b_B, bsrsBs[j]], [pB])
                tt("dve", mixT[:, j, 512:576], pv[:, 0:64], usg[:, 512:576], ALU.mult, [pB, usgB_], [mxB[j][1]])

        if stop == "A":
            return nc, S
        mark("B%d" % p)
        fence(a_par + a_sub)
        sg_bc, sg_bcB = load_bc(ssm_norm_g)
        slot_z = [w_in_block(SPLITS["z"] + zb * 512) for zb in range(2)]

        def xact(c):
            t_, b_ = wk[c // 3]
            return bf(t_)[:, (c % 3) * 576:(c % 3) * 576 + 576], b_

        craw, crawB = wk[6]
        for xb in range(3):
            slot, slotB = w_in_block(SPLITS["xbc"] + xb * 512)
            for cc in range(4):
                c = xb * 4 + cc
                xraw, xrawB = wk[(4, 7, 9)[c % 3]]
                acc, accB = wk[(5, 8, 10)[c % 3]]
                xa, xaB = xact(c)
                pv, pB = bank()
                for k in range(8):
                    mm(pv[:, 0:512], slot[:, k, cc * 128:(cc + 1) * 128], hnT[:, k, 0:512], k == 0, k == 7, hnTB[0:4] + [slotB], [pB])
                cp("dve", xraw[:, 0:3], convhist[:, c, :], [convhist_B], [xrawB])
                cp("act", xraw[:, 3:515], pv[:, 0:512], [pB], [xrawB])
                cp("dve", convhist[:, c, :], xraw[:, 512:515], [xrawB], [convhist_B])
                ts("dve", acc[:, 0:512], xraw[:, 0:512], cwc[:, c, 1:2], cwc[:, c, 0:1], ALU.mult, ALU.add, [xrawB, cwc_B], [accB])
                for kk in range(1, 4):
                    stt("dve", acc[:, 0:512], xraw[:, kk:kk + 512], cwc[:, c, 1 + kk:2 + kk], acc[:, 0:512], ALU.mult, ALU.add,
                        [xrawB, cwc_B, accB], [accB])
                act(xa[:, 0:512], acc[:, 0:512], AF.Silu, [accB], [xaB])
                if p == PS_B:
                    pv, pB = bank()
                    for k in range(8):
                        mm(pv[:, 0:64], slot[:, k, cc * 128:(cc + 1) * 128], hnT[:, k, 512:576], k == 0, k == 7, [hnTB[4], slotB], [pB])
                    xrs = xraw[:, 520:632].rearrange("p (i k) -> p i k", k=7)
                    cp("dve", xrs[:, :, 0:3], scT[:, c, :].rearrange("p (i k) -> p i k", k=3), [scT_B], [xrawB])
                    cp("act", xrs[:, :, 3:7], pv[:, 0:64].rearrange("p (i k) -> p i k", k=4), [pB], [xrawB])
                    accs = acc[:, 512:576].rearrange("p (i k) -> p i k", k=4)
                    ts("dve", accs, xrs[:, :, 0:4], cwc[:, c, 1:2], cwc[:, c, 0:1], ALU.mult, ALU.add, [xrawB, cwc_B], [accB])
                    for kk in range(1, 4):
                        stt("dve", accs, xrs[:, :, kk:kk + 4], cwc[:, c, 1 + kk:2 + kk], accs, ALU.mult, ALU.add,
                            [xrawB, cwc_B, accB], [accB])
                    act(xa[:, 512:576], acc[:, 512:576], AF.Silu, [accB], [xaB])
            if p == NPASS - 1:
                pv, pB = bank()
                for k in range(8):
                    mm(pv[:, 0:512], hnT[:, k, 384:512], slot[:, k, :], k == 0, k == 7, [hnTB[3], slotB], [pB])
                cp("act", craw[:, 0:512], pv[:, 0:512], [pB], [crawB])
                S.dma("sp", o_conv_p[:, xb * 512:(xb + 1) * 512], craw[125:128, 0:512], reads=[crawB])
            if p == PS_B:
                pv, pB = bank()
                for k in range(8):
                    mm(pv[0:64, 0:512], hnT[:, k, 512:576], slot[:, k, :], k == 0, k == 7, [hnTB[4], slotB], [pB])
                cp("act", craw[0:64, 512:1024], pv[0:64, 0:512], [pB], [crawB])
                ocs = o_conv_s.rearrange("(i k) c -> k i c", k=3)
                for kk in range(3):
                    S.dma("sp", ocs[kk, :, xb * 512:(xb + 1) * 512], craw[1 + kk:64:4, 512:1024], reads=[crawB])

        mark("Bt%d" % p)
        def ssd_tile(i, c0, L, smp):
            ws_ = [4, 5, 6, 7, 8, 9, 10, 11, 12] if i % 2 == 0 else [13, 14, 15, 16, 17, 18, 19, 20, 21]
            ss_ = [1, 2, 3, 4] if i % 2 == 0 else [8, 9, 10, 11]
            ba_ = 0 if i % 2 == 0 else 2
            bctr_ = [0]
            b2ctr_ = [0]

            def bank():
                k_ = ba_ + bctr_[0] % 2
                bctr_[0] += 1
                return pp[k_ // 2][:, (k_ % 2) * 512:(k_ % 2) * 512 + 512], PB[k_]

            def bank_bf():
                k_ = ba_ + bctr_[0] % 2
                bctr_[0] += 1
                return ppb[k_ // 2][:, (k_ % 2) * 1024:(k_ % 2) * 1024 + 1024], PB[k_]

            def bank2(kind=None):
                if kind == "py":
                    q_ = 2
                elif kind == "po":
                    q_ = 3
                elif kind == "own":
                    q_ = ba_ // 2
                else:
                    q_ = b2ctr_[0] % 2
                    b2ctr_[0] += 1
                return pp[q_], [PB[2 * q_], PB[2 * q_ + 1]]
            Um, UmB = (utri_s, utri_s_B) if smp else (utri, utri_B)
            On, OnB = (same_s, same_s_B) if smp else (ones_f, ones_f_B)
            dl, dlB = (d96s, d96s_B) if smp else (d96, d96_B)
            mk_, mkB = (mask_s, mask_s_B) if smp else (mask_p, mask_p_B)
            t6, t6B = wk[ws_[2]]
            xs_tm = bf(t6)[:, 0:1024]
            bm_tm = bf(t6)[:, 1024:1280]
            t7, t7B = wk[ws_[3]]
            xdt = bf(t7)[:, 0:1024]
            xdtE = bf(t7)[:, 1024:2048]
            t8, t8B = wk[ws_[4]]
            xsD = bf(t8)[:, 0:1024]
            cbTs = bf(t8)[:, 1024:1280]
            t9, t9B = wk[ws_[5]]
            hTb = bf(t9)[:, 0:1024]
            hbB = hTb_bufs[i % 2]
            ybB = yb_bufs[i % 2]
            yb = bf(t9)[:, 1024:2048]
            t10, t10B = wk[ws_[6]]
            G = bf(t10)
            zs, zsB = wk[ws_[7]]
            yy, yyB = wk[ws_[8]]
            acs, acsB = wk[ws_[0]]
            r1t, r1B = wk[ws_[1]]
            s1, s1B = sm[ss_[0]]
            s2, s2B = sm[ss_[1]]
            s3, s3B = sm[ss_[2]]
            s4, s4B = sm[ss_[3]]
            pvb, pB = bank_bf()
            for c in range(8):
                xa, xaB = xact(c)
                tr(pvb[:L, c * 128:(c + 1) * 128], xa[:, c0:c0 + L], ident_b[:, :], [xaB, ident_b_B], [pB])
            cp("act", xs_tm[:L, :], pvb[:L, 0:1024], [pB], [t6B])
            pvb, pB = bank_bf()
            for c in range(2):
                xa, xaB = xact(8 + c)
                tr(pvb[:L, c * 128:(c + 1) * 128], xa[:, c0:c0 + L], ident_b[:, :], [xaB, ident_b_B], [pB])
            cp("act", bm_tm[:L, :], pvb[:L, 0:256], [pB], [t6B])
            yield
            pv, pB = bank()
            for k in range(8):
                mm(pv[:L, 0:16], hnT[:, k, c0:c0 + L], wdt[:, k, :], k == 0, k == 7, [hnTB[i], wdt_B], [pB])
            tt("dve", s1[:L, 0:16], pv[:L, 0:16], hvec[:L, 0, :], ALU.add, [pB, hvec_B], [s1B])
            stt("dve", s1[:L, 16:32], s1[:L, 0:16], -1.0, s1[:L, 0:16], ALU.mult, ALU.max, [s1B], [s1B])
            act(s1[:L, 16:32], s1[:L, 16:32], AF.Exp, [s1B], [s1B], scale=-1.0)
            act(s1[:L, 16:32], s1[:L, 16:32], AF.Ln, [s1B], [s1B], bias=1.0)
            stt("dve", s1[:L, 0:16], s1[:L, 0:16], 0.0, s1[:L, 16:32], ALU.max, ALU.add, [s1B], [s1B])
            tt("dve", s1[:L, 32:48], s1[:L, 0:16], hvec[:L, 1, :], ALU.mult, [s1B, hvec_B], [s1B])
            yield
            pc0, pc0B = bank()
            dz, dzB = dtAz[i % 2]
            ts1("dve", dz[:L, :].rearrange("p (g b k) -> p g b k", g=4, b=3)[:, :, :, 0:4],
                s1[:L, 32:48].rearrange("p (g k) -> p g k", g=4).unsqueeze(2).to_broadcast([L, 4, 3, 4]), -1.0, ALU.mult, [s1B], [dzB])
            for g in range(4):
                mm(pc0[0:96, g * 128:g * 128 + L], dz[:L, g * 96:(g + 1) * 96], Um[:L, :L], True, True, [dzB, UmB], [pc0B])
            pc1, pc1B = bank()
            mm(pc1[:L, 0:16], Um[:L, :L], s1[:L, 32:48], True, True, [s1B, UmB], [pc1B])
            mm(pc1[:L, 16:32], On[:L, :L], s1[:L, 32:48], True, True, [s1B, OnB], [pc1B])
            cp("act", s2[:L, 0:32], pc1[:L, 0:32], [pc1B], [s2B])
            tt("dve", s3[:L, 32:48], s2[:L, 16:32], s2[:L, 0:16], ALU.subtract, [s2B], [s3B])
            act(s3[:L, 32:48], s3[:L, 32:48], AF.Exp, [s3B], [s3B])
            act(s3[:L, 0:32], s2[:L, 0:32], AF.Exp, [s2B], [s3B])
            tt("dve", s3[:L, 48:64], s1[:L, 0:16], s3[:L, 32:48], ALU.mult, [s1B, s3B], [s3B])
            def v3(ap):
                return ap.rearrange("p (g l) -> p g l", g=4)[:, :, 0:L]
            Tb = bf(r1t)
            r1hB = r1h_bufs[i % 2]
            cp("act", v3(acs[0:96, 0:512]), v3(pc0[0:96, 0:512]), [pc0B], [acsB])
            cp("dve", v3(Tb[0:96, 0:512]), v3(acs[0:96, 0:512]), [acsB], [r1B])
            for q0 in (32, 64):
                tt("dve", v3(acs[q0:q0 + 32, 512:1024]), v3(acs[q0:q0 + 32, 0:512]), v3(Tb[q0:q0 + 32, 0:512]), ALU.subtract, [acsB, r1B], [acsB])
            for q0 in (32, 64):
                cp("dve", v3(Tb[q0:q0 + 32, 0:512]), v3(acs[q0:q0 + 32, 512:1024]), [acsB], [r1B])
            tt("dve", v3(acs[64:96, 512:1024]), v3(acs[64:96, 512:1024]), v3(Tb[64:96, 0:512]), ALU.subtract, [acsB, r1B], [acsB])
            cp("dve", v3(Tb[64:96, 0:512]), v3(acs[64:96, 512:1024]), [acsB], [r1B])
            yield
            for g in range(4):
                r1 = Tb[0:96, 1024 + (g % 2) * 512:1024 + (g % 2) * 512 + 4 * L]
                tt("pool", r1.rearrange("p (k l) -> p k l", k=4), dl[0:96, 0:4 * L].rearrange("p (k l) -> p k l", k=4),
                   Tb[0:96, g * 128:g * 128 + L].unsqueeze(1).to_broadcast([96, 4, L]), ALU.mult, [dlB, r1B], [r1hB[g % 2]])
                pd, pdB = bank()
                mm(pd[:L, 0:4 * L], nones_b[0:96, 0:L], r1, True, False, [nones_b_B, r1hB[g % 2], r1B], [pdB])
                mm(pd[:L, 0:4 * L], Tb[0:96, g * 128:g * 128 + L], dl[0:96, 0:4 * L], False, False, [r1B, dlB], [pdB])
                mm(pd[:L, 0:4 * L], ident_b[:L, :L], mk_[:L, 0:4 * L], False, True, [ident_b_B, mkB], [pdB])
                act(G[:L, g * 4 * L:(g + 1) * 4 * L], pd[:L, 0:4 * L], AF.Exp, [pdB], [t10B])
            yield
            pv, pB = bank()
            for g2 in range(2):
                bT, bTB = xact(8 + g2)
                cT, cTB = xact(10 + g2)
                mm(pv[:L, g2 * L:(g2 + 1) * L], bT[:, c0:c0 + L], cT[:, c0:c0 + L], True, True, [bTB, cTB], [pB])
            cp("act", cbTs[:L, 0:2 * L], pv[:L, 0:2 * L], [pB], [t8B])
            yield
            for g2 in range(2):
                Gv = G[:L, g2 * 8 * L:(g2 + 1) * 8 * L].rearrange("p (h l) -> p h l", h=8)
                tt("dve", Gv, Gv, cbTs[:L, g2 * L:(g2 + 1) * L].unsqueeze(1).to_broadcast([L, 8, L]), ALU.mult, [t10B, t8B], [t10B])
            yield
            xs3 = xs_tm[:L, :].rearrange("p (h q) -> p h q", h=16)
            tt("dve", xdt[:L, :].rearrange("p (h q) -> p h q", h=16), xs3, s1[:L, 0:16].unsqueeze(2).to_broadcast([L, 16, 64]),
               ALU.mult, [t6B, s1B], [t7B])
            tt("dve", xdtE[:L, :].rearrange("p (h q) -> p h q", h=16), xs3, s3[:L, 48:64].unsqueeze(2).to_broadcast([L, 16, 64]),
               ALU.mult, [t6B, s3B], [t7B])
            tt("dve", xsD[:L, :].rearrange("p (h q) -> p h q", h=16), xs3, dsk_b[:L, :].unsqueeze(2).to_broadcast([L, 16, 64]),
               ALU.mult, [t6B, dsk_b_B], [t8B])
            yield
            for zb in range(2):
                slt, sltB = slot_z[zb]
                pv, pB = bank()
                for k in range(8):
                    mm(pv[:L, 0:512], hnT[:, k, c0:c0 + L], slt[:, k, :], k == 0, k == 7, [hnTB[i], sltB], [pB])
                act(zs[:L, zb * 512:(zb + 1) * 512], pv[:L, 0:512], AF.Silu, [pB], [zsB])
            if not smp:
                cp("act", hTb[:, :], hT[:, :], [hT_B], [hbB])
                psn, psnB = bank2("own")
                for g2 in range(2):
                    mm(psn[:, g2 * 512:(g2 + 1) * 512], bm_tm[:L, g2 * 128:(g2 + 1) * 128], xdtE[:L, g2 * 512:(g2 + 1) * 512], True, True,
                       [t6B, t7B], [psnB[g2]])
                tt("dve", hT[:].rearrange("p (h q) -> p h q", h=16), hT[:].rearrange("p (h q) -> p h q", h=16),
                   s3[:, 16:32].unsqueeze(2).to_broadcast([128, 16, 64]), ALU.mult, [hT_B, s3B], [hT_B])
                tt("dve", hT[:, :], hT[:, :], psn[:, :], ALU.add, [hT_B] + psnB, [hT_B])
            yield "B"
            yield
            py, pyB = bank2("py")
            pin(pyB)
            for g2 in range(2):
                mm(py[:L, g2 * 512:(g2 + 1) * 512], ident_b[:L, :L], xsD[:L, g2 * 512:(g2 + 1) * 512], True, True,
                   [ident_b_B, t8B], [pyB[g2]], skip=True)
            for h in range(16):
                mm(py[:L, h * 64:(h + 1) * 64], G[:L, h * L:(h + 1) * L], xdt[:L, h * 64:(h + 1) * 64], False, True,
                   [t10B, t7B], [pyB[h // 8]], skip=True)
            yield
            po, poB = bank2("po")
            pin(poB)
            if not smp:
                for g2 in range(2):
                    cT, cTB = xact(10 + g2)
                    mm(po[:L, g2 * 512:(g2 + 1) * 512], cT[:, c0:c0 + L], hTb[:, g2 * 512:(g2 + 1) * 512], True, True,
                       [cTB, hbB], [poB[g2]])
            else:
                fence([wk[18][1], hTb_bufs[1], yb_bufs[1]])
                cmm, cmmB = wk[13]
                cmmv = bf(cmm).rearrange("p (g i t) -> p g i t", g=2, i=16)
                for g2 in range(2):
                    cT, cTB = xact(10 + g2)
                    tt("dve", cmmv[:, g2], seqm[:], cT[:, 512:576].unsqueeze(1).to_broadcast([128, 16, 64]), ALU.mult,
                       [seqm_B, cTB], [cmmB])
                bmmv = []
                for hf in range(2):
                    bt, btB = wk[14 + hf]
                    v_ = bf(bt)[0:64, :].rearrange("p (i c) -> p i c", i=8)
                    tt("dve", v_, bm_tm[0:64, 0:256].unsqueeze(1).to_broadcast([64, 8, 256]),
                       seqm_t[:, hf * 8:(hf + 1) * 8].unsqueeze(2).to_broadcast([64, 8, 256]), ALU.mult, [t6B, seqm_t_B], [btB])
                    bmmv.append((v_, btB))
                x18, x18B = wk[18]
                h0T = bf(x18)[:, 0:1024]
                pdp, pdpB = bank()
                for hh in range(2):
                    rh = x18[0:64, 512 + hh * 128:512 + (hh + 1) * 128]
                    tt("dve", rh.rearrange("p (i a) -> p i a", i=16), seqm_t[:, :].unsqueeze(2).to_broadcast([64, 16, 8]),
                       s1[0:64, 32:48].rearrange("p (a h) -> p h a", h=2)[:, hh, :].unsqueeze(1).to_broadcast([64, 16, 8]),
                       ALU.mult, [seqm_t_B, s1B], [x18B])
                    mm(pdp[hh * 64:(hh + 1) * 64, 0:128], ones_f[0:64, 0:64], rh, True, True, [ones_f_B, x18B], [pdpB])
                decP = x18[:, 768:896]
                act(decP, pdp[:, 0:128], AF.Exp, [pdpB], [x18B])
                h0s_ = [wk[16], wk[19]]
                hns_ = [wk[17], wk[20]]
                h0Ts_ = [(h0T, x18B), (bf(wk[21][0])[:, 0:1024], wk[21][1])]
                h0b_ = bf(wk[21][0])[:, 1024:2048]
                h0bB_ = wk[21][1]
                for sq in range(16):
                    h0, h0B = h0s_[sq % 2]
                    hn, hnB = hns_[sq % 2]
                    h0Tq, h0TqB = h0Ts_[sq % 2]
                    S.dma("sp", h0[:].rearrange("p (a n) -> p a n", a=8), st_ssm[sq].rearrange("a p n -> p a n"), writes=[h0B])
                    cp("act", h0b_[:, :], h0[:, :], [h0B], [h0bB_])
                    p2b_, p2bB_ = bank_bf()
                    for a in range(8):
                        tr(p2b_[:, a * 128:(a + 1) * 128], h0b_[:, a * 128:(a + 1) * 128], ident_b[:, :], [h0bB_, ident_b_B], [p2bB_])
                    cp("dve", h0Tq[:, :], p2b_[:, 0:1024], [p2bB_], [h0TqB])
                    for g2 in range(2):
                        mm(po[0:64, g2 * 512:(g2 + 1) * 512], cmmv[:, g2, sq, :], h0Tq[:, g2 * 512:(g2 + 1) * 512], sq == 0, sq == 15,
                           [cmmB, h0TqB], [poB[g2]])
                    ps2, ps2B = bank2()
                    bv, bvB = bmmv[sq // 8]
                    for a in range(8):
                        mm(ps2[:, a * 128:(a + 1) * 128], xdtE[0:64, a * 128:(a + 1) * 128], bv[:, sq % 8, (a // 4) * 128:(a // 4) * 128 + 128],
                           True, True, [t7B, bvB], [ps2B[a // 4]])
                    for a in range(8):
                        stt("dve", hn[:, a * 128:(a + 1) * 128], h0[:, a * 128:(a + 1) * 128], decP[:, sq * 8 + a:sq * 8 + a + 1],
                            ps2[:, a * 128:(a + 1) * 128], ALU.mult, ALU.add, [h0B, x18B, ps2B[a // 4]], [hnB])
                    S.dma("sp", o_ssm_s[sq].rearrange("a p n -> p a n"), hn[:].rearrange("p (a n) -> p a n", a=8), reads=[hnB])
            yield
            tt("dve", yy[:L, :].rearrange("p (h q) -> p h q", h=16), po[:L, :].rearrange("p (h q) -> p h q", h=16),
               s3[:L, 0:16].unsqueeze(2).to_broadcast([L, 16, 64]), ALU.mult, poB + [s3B], [yyB])
            tt("dve", yy[:L, :], py[:L, :], yy[:L, :], ALU.add, pyB + [yyB], [yyB])
            unpin(pyB)
            unpin(poB)
            yield
            tt("dve", yy[:L, :], yy[:L, :], zs[:L, :], ALU.mult, [yyB, zsB], [yyB])
            yield
            for g2 in range(2):
                mean_var(yy[:L, g2 * 512:(g2 + 1) * 512], L, 512, yyB, s4, s4B, 2 * g2)
                stt("dve", s4[:L, 2 * g2 + 1:2 * g2 + 2], s4[:L, 2 * g2:2 * g2 + 1], s4[:L, 2 * g2:2 * g2 + 1], s4[:L, 2 * g2 + 1:2 * g2 + 2],
                    ALU.mult, ALU.add, [s4B], [s4B])
                act(s4[:L, 8 + g2:9 + g2], s4[:L, 2 * g2 + 1:2 * g2 + 2], AF.Ln, [s4B, eps_c_B], [s4B], bias=eps_c[:L, :])
                act(s4[:L, 8 + g2:9 + g2], s4[:L, 8 + g2:9 + g2], AF.Exp, [s4B], [s4B], scale=-0.5)
                stt("dve", yb[:L, g2 * 512:(g2 + 1) * 512], yy[:L, g2 * 512:(g2 + 1) * 512], s4[:L, 8 + g2:9 + g2],
                    sg_bc[:L, g2 * 512:(g2 + 1) * 512], ALU.mult, ALU.mult, [yyB, s4B, sg_bcB], [ybB])
            yield
            pvb, pB = ppb[3][:, 0:1024], PB[6]
            for c in range(8):
                tr(pvb[:, c * L:(c + 1) * L], yb[:L, c * 128:(c + 1) * 128], ident_b[:L, :L], [ybB, ident_b_B], [pB])
            cp("act", mixT[:, 8:16, c0:c0 + L], pvb[:, 0:8 * L].rearrange("p (k l) -> p k l", k=8), [pB], [mxB[k_][1 if smp else 0] for k_ in range(8, 16)])
            yield

        t9_all = [wk[9][1], wk[18][1]] + hTb_bufs + yb_bufs
        fence(t9_all)
        gens = [ssd_tile(i_, *t_) for i_, t_ in enumerate(tilesB)]
        prevg = None
        for g_ in gens:
            if prevg is None:
                for v_ in g_:
                    if v_ == "B":
                        break
            else:
                a_done = b_done = False
                while not (a_done and b_done):
                    if not b_done:
                        try:
                            next(prevg)
                        except StopIteration:
                            b_done = True
                    if not a_done:
                        try:
                            if next(g_) == "B":
                                a_done = True
                        except StopIteration:
                            a_done = True
            prevg = g_
        for v_ in prevg:
            pass
        fence(t9_all)
        if p == NPASS - 1:
            p2, p2B = bank2()
            for a in range(8):
                tr(p2[:, a * 128:(a + 1) * 128], hT[:, a * 128:(a + 1) * 128], ident_f[:, :], [hT_B, ident_f_B], [p2B[a // 4]])
            so, soB = wk[12]
            for hb in range(2):
                cp("act", so[:, hb * 512:(hb + 1) * 512], p2[:, hb * 512:(hb + 1) * 512], [p2B[hb]], [soB])
            S.dma("sp", o_ssm_p.rearrange("a p n -> p a n"), so[:].rearrange("p (a n) -> p a n", a=8), reads=[soB])

        if stop == "B":
            return nc, S
        mark("C%d" % p)
        if p == 0:
            phase_M()
        def qc(c):
            t_, b_ = wk[c // 3]
            return bf(t_)[:, (c % 3) * 576:(c % 3) * 576 + 576], b_

        def gc(c):
            t_, b_ = wk[3 + c // 3]
            return bf(t_)[:, (c % 3) * 576:(c % 3) * 576 + 576], b_

        for kind in range(2):
            for blk in range(2):
                slot, slotB = w_in_block((SPLITS["q"] if kind == 0 else SPLITS["mg"]) + blk * 512)
                for cc in range(4):
                    c = blk * 4 + cc
                    dst, dstB = qc(c) if kind == 0 else gc(c)
                    pv, pB = bank()
                    for k in range(8):
                        mm(pv[:, 0:512], slot[:, k, cc * 128:(cc + 1) * 128], hnT[:, k, 0:512], k == 0, k == 7, hnTB[0:4] + [slotB], [pB])
                    if kind == 0:
                        S.add("act", lambda e, a=dst[:, 0:512], b=pv[:, 0:512]: e.mul(a, b, 0.0625), [pB], [dstB])
                    else:
                        act(dst[:, 0:512], pv[:, 0:512], AF.Silu, [pB], [dstB])
                    if p == PS_C:
                        pv, pB = bank()
                        for k in range(8):
                            mm(pv[:, 0:64], slot[:, k, cc * 128:(cc + 1) * 128], hnT[:, k, 512:576], k == 0, k == 7, [hnTB[4], slotB], [pB])
                        if kind == 0:
                            S.add("act", lambda e, a=qTs[:, c, :], b=pv[:, 0:64]: e.mul(a, b, 0.0625), [pB], [qTs_B])
                        else:
                            act(mgss[:, c, :], pv[:, 0:64], AF.Silu, [pB], [mgss_B])
        pTt, pTB = wk[6]
        pTv = bf(pTt)
        rst, rsB = wk[7]
        for h in range(4):
            psum_, psumB = bank()
            pin(psumB)
            po2, po2B = bank2()
            pin(po2B)
            for mt in range(2):
                pv, pB = bank()
                for dc in range(2):
                    q_, qB_ = qc(2 * h + dc)
                    mm(pv[:, 0:512], kT[:, 2 * h + dc, mt * 128:(mt + 1) * 128], q_[:, 0:512], dc == 0, dc == 1, [kT_B, qB_], [pB])
                act(pTv[:, mt * 512:(mt + 1) * 512], pv[:, 0:512], AF.Exp, [pB], [pTB])
                mm(psum_[:, 0:512], ones_b[:, :], pTv[:, mt * 512:(mt + 1) * 512], mt == 0, mt == 1, [ones_b_B, pTB], [psumB])
                for dc in range(2):
                    mm(po2[:, dc * 512:(dc + 1) * 512], v_tm[:, mt, (2 * h + dc) * 128:(2 * h + dc + 1) * 128], pTv[:, mt * 512:(mt + 1) * 512],
                       mt == 0, mt == 1, [v_tm_B, pTB], [po2B[dc]])
            S.add("dve", lambda e, a=rst[:, 0:512], b=psum_[:, 0:512]: e.reciprocal(a, b), [psumB], [rsB])
            for dc in range(2):
                g_, gB_ = gc(2 * h + dc)
                tt("dve", rst[:, 512:1024], po2[:, dc * 512:(dc + 1) * 512], rst[:, 0:512], ALU.mult, [po2B[dc], rsB], [rsB])
                tt("dve", mixT[:, 16 + 2 * h + dc, 0:512], rst[:, 512:1024], g_[:, 0:512], ALU.mult, [rsB, gB_], [mxB[16 + 2 * h + dc][0]])
            unpin(psumB)
            unpin(po2B)
        if p == PS_C:
            for sq in range(16):
                s5, s5B = sm[(5, 12)[sq % 2]]
                s6, s6B = sm[(6, 13)[sq % 2]]
                pTs = s5[:].bitcast(BF16)
                Kt, KB = wk[8 + sq % 2]
                Vt, VB = wk[10 + sq % 2]
                Kb = bf(Kt).rearrange("p (m e) -> p m e", m=2)
                Vb = bf(Vt).rearrange("p (m e) -> p m e", m=2)
                S.dma("pool", Kb, ck[sq].rearrange("(m p) e -> p m e", p=128), writes=[KB])
                S.dma("pool", Vb, cv[sq].rearrange("(m p) e -> p m e", p=128), writes=[VB])
                p2i = bank2_bf()
                p2b, p2B = p2i
                for c in range(8):
                    for mt in range(2):
                        tr(p2b[:, c * 256 + mt * 128:c * 256 + mt * 128 + 128], Kb[:, mt, c * 128:(c + 1) * 128], ident_b[:, :],
                           [KB, ident_b_B], [p2B[c // 4]])
                kts_t, ktsB = wk[(12, 13)[sq % 2]]
                kTs = bf(kts_t).rearrange("p (c m) -> p c m", c=8)
                for hb in range(2):
                    cp("dve", kTs[:, hb * 4:(hb + 1) * 4, :], p2b[:, hb * 1024:(hb + 1) * 1024].rearrange("p (c m) -> p c m", c=4),
                       [p2B[hb]], [ktsB])
                pv, pB = bank()
                for h in range(4):
                    for mt in range(2):
                        for dc in range(2):
                            mm(pv[:, (h * 2 + mt) * 4:(h * 2 + mt) * 4 + 4], kTs[:, 2 * h + dc, mt * 128:(mt + 1) * 128],
                               qTs[:, 2 * h + dc, 4 * sq:4 * sq + 4], dc == 0, dc == 1, [ktsB, qTs_B], [pB])
                act(pTs[:, 0:32], pv[:, 0:32], AF.Exp, [pB], [s5B])
                pv2, pB2 = bank()
                for h in range(4):
                    for mt in range(2):
                        mm(pv2[:, h * 4:(h + 1) * 4], ones_b[:, :], pTs[:, (h * 2 + mt) * 4:(h * 2 + mt) * 4 + 4], mt == 0, mt == 1,
                           [ones_b_B, s5B], [pB2])
                for c in range(8):
                    h = c // 2
                    for mt in range(2):
                        mm(pv2[:, 64 + c * 4:64 + c * 4 + 4], Vb[:, mt, c * 128:(c + 1) * 128], pTs[:, (h * 2 + mt) * 4:(h * 2 + mt) * 4 + 4],
                           mt == 0, mt == 1, [VB, s5B], [pB2])
                S.add("dve", lambda e, a=s6[:, 0:16], b=pv2[:, 0:16]: e.reciprocal(a, b), [pB2], [s6B])
                tt("dve", s6[:, 16:48].rearrange("p (h d t) -> p h d t", h=4, d=2), pv2[:, 64:96].rearrange("p (h d t) -> p h d t", h=4, d=2),
                   s6[:, 0:16].rearrange("p (h t) -> p h t", h=4).unsqueeze(2).to_broadcast([128, 4, 2, 4]), ALU.mult, [pB2, s6B], [s6B])
                tt("dve", mixT[:, 16:24, 512 + 4 * sq:512 + 4 * sq + 4], s6[:, 16:48].rearrange("p (c t) -> p c t", c=8),
                   mgss[:, :, 4 * sq:4 * sq + 4], ALU.mult, [s6B, mgss_B], [mxB[k_][1] for k_ in range(16, 24)])

        if stop == "C":
            return nc, S
        mark("D%d" % p)
        fn_bc, fn_bcB = load_bc(fng)
        for i, (c0, L, smp) in enumerate(tilesD):
            src = x_s[:, :] if smp else x_p[p * 512 + c0:p * 512 + c0 + 128, :]
            S.dma("sp", wk[i][0][:L, :], src, writes=[wk[i][1]])
        for half in range(2):
            slots = [load_w(w_out[s_ * 1024:(s_ + 1) * 1024, half * 512:(half + 1) * 512], 8, 512, key=("out", s_, half)) for s_ in range(3)]
            for i, (c0, L, smp) in enumerate(tilesD):
                yt, ytB = wk[i]
                pv, pB = bank()
                for k in range(24):
                    slt, sltB = slots[k // 8]
                    mm(pv[:L, 0:512], mixT[:, k, c0:c0 + L], slt[:, k % 8, :], k == 0, k == 23, [mxB[k][1 if smp else 0], sltB], [pB])
                tt("dve", yt[:L, half * 512:(half + 1) * 512], pv[:L, 0:512], yt[:L, half * 512:(half + 1) * 512], ALU.add, [pB, ytB], [ytB])
        for i, (c0, L, smp) in enumerate(tilesD):
            yt, ytB = wk[i]
            ot, otB = wk[5 + i % 2]
            st, stB = sm[(0, 7)[i % 2]]
            mean_var(yt[:L, :], L, 1024, ytB, st, stB, 0)
            rstd_from(st, stB, L, 1, 2, use_mean_col=0)
            stt("dve", ot[:L, :], yt[:L, :], st[:L, 2:3], fn_bc[:L, :], ALU.mult, ALU.mult, [ytB, stB, fn_bcB], [otB])
            dst = y_s[:, :] if smp else y_p[p * 512 + c0:p * 512 + c0 + 128, :]
            S.dma("sp", dst, ot[:L, :], reads=[otB])
        if stop == "D":
            return nc, S

    return nc, S


def _consts():
    f = np.float32
    c = {}
    c["c_ident"] = np.eye(128, dtype=f)
    i = np.arange(128)
    c["c_utri"] = (i[:, None] <= i[None, :]).astype(f)
    j = np.arange(64)
    same = (j[:, None] // 4 == j[None, :] // 4)
    c["c_same_s"] = same.astype(f)
    c["c_utri_s"] = (same & (j[:, None] <= j[None, :])).astype(f)
    mp = np.where(i[None, :] >= i[:, None], 0.0, NEG).astype(f)
    c["c_mask_p"] = np.tile(mp, (1, 4))
    ms = np.where(same & (j[None, :] >= j[:, None]), 0.0, NEG).astype(f)
    c["c_mask_s"] = np.tile(ms, (1, 4))
    d96 = np.zeros((96, 4, 128), f)
    d96s = np.zeros((96, 4, 64), f)
    for r in range(96):
        if r % 32 < 4:
            d96[r, r % 32, :] = 1.0
            d96s[r, r % 32, :] = 1.0
    c["c_d96"] = d96.reshape(96, 512)
    c["c_d96s"] = d96s.reshape(96, 256)
    sq = (np.arange(16)[:, None] == (j[None, :] // 4)).astype(f)
    c["c_seqm"] = sq.reshape(1, 1024)
    c["c_seqm_t"] = np.ascontiguousarray(sq.T)
    r16 = np.arange(16)
    c["c_oh16"] = (r16[:, None] % 8 == np.arange(8)[None, :]).astype(f)
    c["c_msel"] = np.stack([(r16 < 8), (r16 >= 8)], 1).astype(f)
    c["c_sel4"] = (np.arange(4)[:, None] == (j[None, :] % 4)).astype(f)
    return c


_PROG = None


def kernel(x_prompt, x_sample, mem_prompt, state_ssm, state_conv, cache_mem_k, cache_mem_v,
           norm_g, w_in, gm_norm_g, gm_norm_b, gm_w_spatial, gm_b_spatial, conv_w, conv_b,
           dt_bias, a_log, d_skip, ssm_norm_g, mem_norm_g, w_mem_k, w_mem_v, w_out, final_norm_g):
    global _PROG
    f = np.float32
    A = lambda a: np.ascontiguousarray(np.asarray(a, dtype=f))
    if _PROG is None:
        nc, S = build_program()
        S.emit()
        _PROG = nc
    nc = _PROG
    consts = _consts()
    shared = dict(
        norm_g=A(norm_g).reshape(1, 1024), w_in=A(w_in)[0], gm_norm_g=A(gm_norm_g).reshape(1, 1024),
        gm_norm_b=A(gm_norm_b).reshape(1, 1024), gm_w=A(gm_w_spatial)[0], gm_bs=A(gm_b_spatial)[0],
        conv_w=A(conv_w)[0], conv_b=A(conv_b).reshape(1, 1536), dt_bias=A(dt_bias).reshape(1, 16),
        a_log=A(a_log).reshape(1, 16), d_skip=A(d_skip).reshape(1, 16), ssm_norm_g=A(ssm_norm_g).reshape(1, 1024),
        mem_norm_g=A(mem_norm_g).reshape(1, 1024), w_mk=A(w_mem_k)[0], w_mv=A(w_mem_v)[0], w_out=A(w_out)[0],
        fng=A(final_norm_g).reshape(1, 1024), **consts)
    xp = A(x_prompt); xs = A(x_sample); mp = A(mem_prompt)
    ss = A(state_ssm)[0]; sc = A(state_conv)[0]; ckk = A(cache_mem_k)[0]; cvv = A(cache_mem_v)[0]
    in_maps = []
    for c in range(NCORES):
        b0 = 16 * c
        m = dict(shared)
        m["x_p"] = xp[c]
        m["x_s"] = xs[b0:b0 + 16].reshape(64, 1024)
        m["mem"] = mp[c]
        m["st_ssm"] = ss[b0:b0 + 16].reshape(16, 8, 128, 128)
        m["st_conv"] = sc[b0:b0 + 16].reshape(48, 1536)
        m["ck"] = ckk[b0:b0 + 16].reshape(16, 256, 1024)
        m["cv"] = cvv[b0:b0 + 16].reshape(16, 256, 1024)
        in_maps.append(m)
    res = run_bass_kernel_spmd(nc, in_maps, core_ids=list(range(NCORES)))
    R = res.results
    cat = lambda k: [np.asarray(R[c][k], dtype=f) for c in range(NCORES)]
    y_prompt = np.stack(cat("y_p"), 0)
    y_sample = np.concatenate([a.reshape(16, 4, 1024) for a in cat("y_s")], 0)
    ssm_prompt = np.stack([a.reshape(16, 64, 128) for a in cat("o_ssm_p")], 0)[None]
    conv_prompt = np.stack(cat("o_conv_p"), 0)[None]
    mem_k = np.stack([a.reshape(256, 4, 256) for a in cat("o_mk")], 0)[None]
    mem_v = np.stack([a.reshape(256, 4, 256) for a in cat("o_mv")], 0)[None]
    ssm_sample = np.concatenate([a.reshape(16, 16, 64, 128) for a in cat("o_ssm_s")], 0)[None]
    conv_sample = np.concatenate([a.reshape(16, 3, 1536) for a in cat("o_conv_s")], 0)[None]
    gv = np.concatenate([a.reshape(16, 4, 1024) for a in cat("o_gv")], 0)[None]
    return (y_prompt, y_sample, ssm_prompt, conv_prompt, mem_k, mem_v, ssm_sample, conv_sample, gv)
```

```python
import contextlib
import numpy as np
import concourse.bass as bass
import concourse.mybir as mybir
from concourse.bass_utils import run_bass_kernel_spmd

F32 = mybir.dt.float32
BF16 = mybir.dt.bfloat16
AF = mybir.ActivationFunctionType
ALU = mybir.AluOpType
AX = mybir.AxisListType

NCORES = 8
EPS = 1e-6
NEG = -30000.0


class Buf:
    __slots__ = ("name", "w", "r", "excl")

    def __init__(self, name="", excl=False):
        self.name = name
        self.w = {}
        self.r = {}
        self.excl = excl


class Op:
    __slots__ = ("eng", "fn", "deps", "needed", "count", "dma", "dsem", "dval", "waits", "sw", "cost", "sdeps", "idx", "tset", "tag", "line")

    def __init__(self, eng, fn, dma=False):
        self.eng = eng
        self.fn = fn
        self.deps = []
        self.needed = False
        self.count = None
        self.dma = dma
        self.dsem = None
        self.dval = None
        self.waits = None
        self.sw = False
        self.cost = 0.3
        self.sdeps = []
        self.idx = 0
        self.tset = None
        self.tag = None


class Sched:
    ENGS = ("pe", "dve", "act", "pool", "sp")

    def __init__(self, nc, n_dma_sems=64, n_hw=40):
        self.nc = nc
        self.ops = {e: [] for e in self.ENGS}
        self.n_dma_sems = n_dma_sems
        self.dma_uses = [0] * n_dma_sems
        self.dma_last = [None] * n_dma_sems
        self.dma_rr = 0
        self.pen = 1.3
        self.nops = 0
        self.sw_rr = 0
        self.n_hw = n_hw

    def add(self, eng, fn, reads=(), writes=(), dma=False, cost=0.3):
        op = Op(eng, fn, dma)
        op.cost = cost
        op.idx = self.nops
        self.nops += 1
        op.tag = getattr(self, "cur_tag", None)
        op.line = None
        deps = []
        xr = [b for b in reads if b.excl]
        if xr:
            reads = [b for b in reads if not b.excl]
            writes = list(writes) + [b for b in xr if b not in writes]
        for b in reads:
            for t in b.w.values():
                deps.append((t, "raw"))
        for b in writes:
            for t in b.w.values():
                deps.append((t, "waw"))
            for lst in b.r.values():
                for t in lst:
                    deps.append((t, "war"))
        if dma:
            if eng == "pool":
                s = self.n_hw + self.sw_rr
                self.sw_rr = (self.sw_rr + 1) % (self.n_dma_sems - self.n_hw)
            else:
                s = self.dma_rr
                self.dma_rr = (self.dma_rr + 1) % self.n_hw
            prev = self.dma_last[s]
            if prev is not None:
                deps.append((prev, "dmasem"))
            self.dma_uses[s] += 1
            op.dsem = s
            op.dval = 16 * self.dma_uses[s]
            self.dma_last[s] = op
            op.needed = True
        for t, kind in deps:
            if t is op:
                continue
            if t.dma:
                op.deps.append(t)
            else:
                if t.eng == eng and not dma:
                    if eng == "pe":
                        op.sdeps.append(t)
                        continue
                t.needed = True
                op.deps.append(t)
        key = ("d", op.dsem) if dma else eng
        for b in reads:
            b.r.setdefault(key, []).append(op)
        for b in writes:
            b.w = {key: op}
            b.r = {}
        self.ops[eng].append(op)
        return op

    def dma(self, q, out, in_, reads=(), writes=()):
        n = 1
        for d in out.shape:
            n *= d
        bpe = 2 if (out.dtype == BF16 and in_.dtype == BF16) else 4
        return self.add(q, lambda e: e.dma_start(out=out, in_=in_), reads, writes, dma=True, cost=2.0 + n * bpe / 100e3)

    def schedule(self):
        import heapq
        allops = [op for e in self.ENGS for op in self.ops[e]]
        succ = {id(op): [] for op in allops}
        indeg = {id(op): 0 for op in allops}
        for op in allops:
            for t in list(op.deps) + list(op.sdeps):
                succ[id(t)].append(op)
                indeg[id(op)] += 1
        LAT = 0.25
        blev = {}
        for op in sorted(allops, key=lambda o: -o.idx):
            m = 0.0
            for s_ in succ[id(op)]:
                v = blev[id(s_)] + LAT
                if v > m:
                    m = v
            blev[id(op)] = m + op.cost
        ready_t = {id(op): 0.0 for op in allops}
        wait_h = {e: [] for e in self.ENGS}
        for op in allops:
            if indeg[id(op)] == 0:
                heapq.heappush(wait_h[op.eng], (0.0, op.idx, op))
        free = {e: 0.0 for e in self.ENGS}
        new = {e: [] for e in self.ENGS}
        left = len(allops)
        cur_tset = [None]
        while left:
            best = None
            for e in self.ENGS:
                h = wait_h[e]
                if not h:
                    continue
                t_now = max(free[e], h[0][0])
                cands = [c for c in heapq.nsmallest(24, h) if c[0] <= t_now + (2.0 if e == "act" else 0.05)]
                cb = None
                for rt, idx, op in cands:
                    pen = self.pen if (e == "act" and op.tset is not None and op.tset != cur_tset[0]) else 0.0
                    key = (-(blev[id(op)] - 40.0 * pen), idx)
                    if cb is None or key < cb[0]:
                        cb = (key, rt, idx, op, pen)
                key, rt, idx, op, pen = cb
                st = max(rt, free[e]) + pen
                if best is None or (st, idx) < (best[0], best[1]):
                    best = (st, idx, op, rt)
            st, idx, op, rt0 = best
            h_ = wait_h[op.eng]
            if h_[0][2] is op:
                heapq.heappop(h_)
            else:
                h_.remove((rt0, idx, op))
                heapq.heapify(h_)
            if op.eng == "act" and op.tset is not None:
                cur_tset[0] = op.tset
            if op.dma:
                free[op.eng] = st + (1.0 if op.eng == "pool" else 0.1)
            else:
                free[op.eng] = st + op.cost
            fin = st + op.cost
            new[op.eng].append(op)
            left -= 1
            for s_ in succ[id(op)]:
                k = id(s_)
                if fin + LAT > ready_t[k]:
                    ready_t[k] = fin + LAT
                indeg[k] -= 1
                if indeg[k] == 0:
                    heapq.heappush(wait_h[s_.eng], (ready_t[k], s_.idx, s_))
        self.ops = new
        self.est = max(free.values())

    def finalize(self, final_eng="sp"):
        for e in self.ENGS:
            c = 0
            for op in self.ops[e]:
                if op.dma:
                    continue
                if op.needed:
                    c += 1
                    op.count = c
        tail = Op(final_eng, None)
        for s in range(self.n_dma_sems):
            if self.dma_last[s] is not None:
                tail.deps.append(self.dma_last[s])
        self.ops[final_eng].append(tail)
        for e in self.ENGS:
            seen = {}
            for op in self.ops[e]:
                need = {}
                for t in op.deps:
                    if t.dma:
                        k = ("d", t.dsem)
                        v = t.dval
                    else:
                        k = t.eng
                        v = t.count
                    if v > need.get(k, 0):
                        need[k] = v
                w = []
                for k, v in need.items():
                    if v > seen.get(k, 0):
                        seen[k] = v
                        w.append((k, v))
                op.waits = w

    def emit(self):
        nc = self.nc
        if getattr(self, "do_sched", True):
            self.schedule()
        self.finalize()
        with contextlib.ExitStack() as st:
            esem = {e: st.enter_context(nc.semaphore("s_" + e)) for e in ("pe", "dve", "act", "pool")}
            dsem = [st.enter_context(nc.semaphore("d%d" % i)) for i in range(self.n_dma_sems)]
            block = st.enter_context(nc.Block())

            def run(ename, eng):
                for op in self.ops[ename]:
                    for k, v in op.waits:
                        if isinstance(k, tuple):
                            eng.wait_ge(dsem[k[1]], v)
                        else:
                            eng.wait_ge(esem[k], v)
                    if op.fn is None:
                        continue
                    ins = op.fn(eng)
                    if op.dma:
                        ins.then_inc(dsem[op.dsem], 16)
                    elif op.needed:
                        ins.then_inc(esem[ename], 1)

            @block.tensor
            def _(eng):
                run("pe", eng)

            @block.vector
            def _(eng):
                run("dve", eng)

            @block.scalar
            def _(eng):
                run("act", eng)

            @block.gpsimd
            def _(eng):
                run("pool", eng)

            @block.sync
            def _(eng):
                run("sp", eng)


SPLITS = dict(u=0, v=1024, gate=2048, z=3072, xbc=4096, dt=5632, q=5648, mg=6672)


def build_program(stop=None):
    nc = bass.Bass("TRN2", target_bir_lowering=False)
    S = Sched(nc)
    S.marks = []

    def mark(lbl):
        S.cur_tag = lbl
        S.marks.append((lbl, len(S.ops['dve']), len(S.ops['act']), len(S.ops['pe'])))

    def din(name, shape):
        return nc.dram_tensor(name, list(shape), F32, kind="ExternalInput").ap()

    def dout(name, shape):
        return nc.dram_tensor(name, list(shape), F32, kind="ExternalOutput").ap()

    x_p = din("x_p", [2048, 1024]); x_s = din("x_s", [64, 1024]); mem = din("mem", [256, 1024])
    st_ssm = din("st_ssm", [16, 8, 128, 128]); st_conv = din("st_conv", [48, 1536])
    ck = din("ck", [16, 256, 1024]); cv = din("cv", [16, 256, 1024])
    norm_g = din("norm_g", [1, 1024]); w_in = din("w_in", [1024, 7696])
    gm_norm_g = din("gm_norm_g", [1, 1024]); gm_norm_b = din("gm_norm_b", [1, 1024])
    gm_w = din("gm_w", [8, 128, 128]); gm_bs = din("gm_bs", [8, 128])
    conv_w = din("conv_w", [4, 1536]); conv_b = din("conv_b", [1, 1536])
    dt_bias = din("dt_bias", [1, 16]); a_log = din("a_log", [1, 16]); d_skip = din("d_skip", [1, 16])
    ssm_norm_g = din("ssm_norm_g", [1, 1024]); mem_norm_g = din("mem_norm_g", [1, 1024])
    w_mk = din("w_mk", [1024, 1024]); w_mv = din("w_mv", [1024, 1024]); w_out = din("w_out", [3072, 1024])
    fng = din("fng", [1, 1024])
    c_ident = din("c_ident", [128, 128]); c_utri = din("c_utri", [128, 128]); c_utri_s = din("c_utri_s", [64, 64])
    c_same_s = din("c_same_s", [64, 64]); c_mask_p = din("c_mask_p", [128, 512]); c_mask_s = din("c_mask_s", [64, 256])
    c_d96 = din("c_d96", [96, 512]); c_d96s = din("c_d96s", [96, 256])
    c_oh16 = din("c_oh16", [16, 8]); c_msel = din("c_msel", [16, 2])
    c_seqm = din("c_seqm", [1, 1024]); c_seqm_t = din("c_seqm_t", [64, 16]); c_sel4 = din("c_sel4", [4, 64])

    y_p = dout("y_p", [2048, 1024]); y_s = dout("y_s", [64, 1024])
    o_ssm_p = dout("o_ssm_p", [8, 128, 128]); o_conv_p = dout("o_conv_p", [3, 1536])
    o_mk = dout("o_mk", [256, 1024]); o_mv = dout("o_mv", [256, 1024])
    o_ssm_s = dout("o_ssm_s", [16, 8, 128, 128]); o_conv_s = dout("o_conv_s", [48, 1536])
    o_gv = dout("o_gv", [64, 1024])

    def sb(name, shape, dt=F32):
        return nc.alloc_sbuf_tensor(name, list(shape), dt), Buf(name)

    NTMAX = 576
    hnT, hnT_B0 = sb("hnT", [128, 8, NTMAX], BF16)
    mixT, mixT_B0 = sb("mixT", [128, 24, NTMAX], BF16)
    hnTB = [Buf("hnT%d" % i) for i in range(5)]
    mxB = [[Buf("mix%d_%d" % (k, q)) for q in range(2)] for k in range(24)]
    NSLOT = 5
    ring = [sb("ring%d" % i, [128, 4096], BF16) for i in range(NSLOT)]
    NWK = 22
    wk = [sb("wk%d" % i, [128, 1024], F32) for i in range(NWK)]

    ident_f, ident_f_B = sb("ident_f", [128, 128]); ident_b, ident_b_B = sb("ident_b", [128, 128], BF16)
    utri, utri_B = sb("utri", [128, 128]); utri_s, utri_s_B = sb("utri_s", [64, 64])
    ones_f, ones_f_B = sb("ones_f", [128, 128]); ones_b, ones_b_B = sb("ones_b", [128, 128], BF16)
    same_s, same_s_B = sb("same_s", [64, 64])
    mask_p, mask_p_B = sb("mask_p", [128, 512], BF16); mask_s, mask_s_B = sb("mask_s", [64, 256], BF16)
    d96, d96_B = sb("d96", [96, 512], BF16); d96s, d96s_B = sb("d96s", [96, 256], BF16)
    nones_b, nones_b_B = sb("nones_b", [128, 128], BF16)
    dtAz = [sb("dtAz%d" % i, [128, 384]) for i in range(2)]
    hTb_bufs = [Buf("hTb0"), Buf("hTb1")]
    yb_bufs = [Buf("yb0"), Buf("yb1")]
    r1h_bufs = [[Buf("r1h%d%d" % (a_, b_)) for b_ in range(2)] for a_ in range(2)]
    seqm, seqm_B = sb("seqm", [128, 16, 64], BF16)
    seqm_t, seqm_t_B = sb("seqm_t", [64, 16])
    sel4, sel4_B = sb("sel4", [4, 64])
    bc = [sb("bc%d" % i, [128, 1024]) for i in range(2)]
    hvec, hvec_B = sb("hvec", [128, 4, 16])
    dsk_b, dsk_b_B = sb("dsk_b", [128, 16], BF16)
    eps_c, eps_c_B = sb("eps_c", [128, 1])
    bs16, bs16_B = sb("bs16", [16, 128], BF16); bs16s, bs16s_B = sb("bs16s", [16, 64], BF16)
    oh16, oh16_B = sb("oh16", [16, 8]); msel, msel_B = sb("msel", [16, 2])
    WT, WT_B = sb("WT", [128, 8, 128], BF16); BD, BD_B = sb("BD", [64, 8, 64], BF16)
    wdt, wdt_B = sb("wdt", [128, 8, 16], BF16)
    kT, kT_B = sb("kT", [128, 8, 256], BF16); v_tm, v_tm_B = sb("v_tm", [128, 2, 1024], BF16)
    hT, hT_B = sb("hT", [128, 1024])
    convhist, convhist_B = sb("convhist", [128, 12, 3])
    scT, scT_B = sb("scT", [128, 12, 48])
    cwc, cwc_B = sb("cwc", [128, 12, 8])
    sm = [sb("sm%d" % i, [128, 64]) for i in range(14)]
    qTs, qTs_B = sb("qTs", [128, 8, 64], BF16); mgss, mgss_B = sb("mgss", [128, 8, 64], BF16)

    pp = [nc.alloc_psum_tensor("pp%d" % i, [128, 1024], F32) for i in range(4)]
    ppb = [t[:].bitcast(BF16) for t in pp]
    PB = [Buf("pb%d" % i, excl=True) for i in range(8)]
    bank_ctr = [0]
    pinned = set()

    def _next1():
        while True:
            i = bank_ctr[0] % 8
            bank_ctr[0] += 1
            if i not in pinned:
                return i

    def _next2():
        while True:
            if bank_ctr[0] % 2:
                bank_ctr[0] += 1
            i = bank_ctr[0] % 8
            bank_ctr[0] += 2
            if i not in pinned and (i + 1) not in pinned:
                return i

    def bank():
        i = _next1()
        return pp[i // 2][:, (i % 2) * 512:(i % 2) * 512 + 512], PB[i]

    def bank_bf():
        i = _next1()
        return ppb[i // 2][:, (i % 2) * 1024:(i % 2) * 1024 + 1024], PB[i]

    def bank2():
        i = _next2()
        return pp[i // 2], [PB[i], PB[i + 1]]

    def bank2_bf():
        i = _next2()
        return ppb[i // 2], [PB[i], PB[i + 1]]

    def pin(bufs):
        for b in (bufs if isinstance(bufs, list) else [bufs]):
            pinned.add(PB.index(b))

    def unpin(bufs):
        for b in (bufs if isinstance(bufs, list) else [bufs]):
            pinned.discard(PB.index(b))

    def fsz(ap):
        n = 1
        for d in ap.shape[1:]:
            n *= d
        return n

    def mm(out, lhsT, rhs, start, stop, r, w, skip=False):
        c = max(fsz(out), 64) / 2400.0 * (4 if lhsT.dtype == F32 else 1) + 0.02
        S.add("pe", lambda e: e.matmul(out, lhsT, rhs, start=start, stop=stop, skip_group_check=skip), r, w, cost=c)

    def tr(out, in_, ident, r, w):
        S.add("pe", lambda e: e.transpose(out, in_, ident), r, w, cost=0.11)

    def act(out, in_, func, r, w, bias=None, scale=None):
        kw = {}
        if bias is not None:
            kw["bias"] = bias
        if scale is not None:
            kw["scale"] = scale
        o_ = S.add("act", lambda e: e.activation(out, in_, func, **kw), r, w, cost=0.25 + fsz(out) / 1200.0)
        o_.tset = {AF.Exp: "EL", AF.Ln: "EL", AF.Silu: "S", AF.Gelu: "G", AF.Sqrt: "Q"}.get(func)

    def ecost(eng, out):
        return 0.15 + fsz(out) / (960.0 if eng == "dve" else 480.0)

    def tt(eng, out, in0, in1, op, r, w):
        S.add(eng, lambda e: e.tensor_tensor(out, in0, in1, op), r, w, cost=ecost(eng, out))

    def ts(eng, out, in0, s1, s2, op0, op1, r, w):
        S.add(eng, lambda e: e.tensor_scalar(out, in0, s1, s2, op0, op1), r, w, cost=ecost(eng, out))

    def ts1(eng, out, in0, s1, op0, r, w):
        S.add(eng, lambda e: e.tensor_single_scalar(out, in0, s1, op0), r, w, cost=ecost(eng, out))

    def stt(eng, out, in0, sc, in1, op0, op1, r, w):
        S.add(eng, lambda e: e.scalar_tensor_tensor(out, in0, sc, in1, op0, op1), r, w, cost=ecost(eng, out))

    def cp(eng, out, in_, r, w):
        if eng == "act":
            S.add("act", lambda e: e.copy(out, in_), r, w, cost=0.25 + fsz(out) / 1200.0)
        else:
            S.add(eng, lambda e: e.tensor_copy(out, in_), r, w, cost=ecost(eng, out))

    def memset(eng, ap, val, w):
        S.add(eng, lambda e: e.memset(ap, val), (), w)

    fscr, fscr_B = sb("fscr", [128, 8])

    def fence(bufs):
        S.add("dve", lambda e: e.memset(fscr[0:1, 0:1], 0.0), (), [fscr_B] + list(bufs), cost=0.1)

    def bf(t):
        return t[:].bitcast(BF16)

    ring_ctr = [0]

    wcache = {}

    def load_w(src_ap, kchunks, ncols, key=None):
        t, b = ring[ring_ctr[0] % NSLOT]
        ring_ctr[0] += 1
        view = t[:, 0:kchunks * ncols].rearrange("p (k c) -> p k c", k=kchunks)
        if key is not None and key in wcache:
            dt_, dB_ = wcache[key]
            S.dma("sp", t[:, 0:kchunks * ncols], dt_, reads=[dB_], writes=[b])
            return view, b
        S.dma("pool", view, src_ap.rearrange("(k p) c -> p k c", p=128), writes=[b])
        if key is not None:
            dt_ = nc.dram_tensor("wc_%s" % "_".join(str(x) for x in key), [128, kchunks * ncols], BF16, kind="Internal").ap()
            dB_ = Buf("wc")
            S.dma("sp", dt_, t[:, 0:kchunks * ncols], reads=[b], writes=[dB_])
            wcache[key] = (dt_, dB_)
        return view, b

    def w_in_block(col0, ncols=512):
        return load_w(w_in[:, col0:col0 + ncols], 8, ncols, key=("in", col0))

    bc_ctr = [0]

    def load_bc(vec):
        t, b = bc[bc_ctr[0] % 2]
        bc_ctr[0] += 1
        S.dma("sp", t[:, :], vec.partition_broadcast(128), writes=[b])
        return t, b

    def mean_var(x_ap, L, width, xB, st, stB, col):
        nchunk = width // 512
        for c in range(nchunk):
            S.add("dve", (lambda c=c: (lambda e: e.bn_stats(st[:L, 32 + 6 * c:38 + 6 * c], x_ap[:, c * 512:(c + 1) * 512])))(), [xB], [stB])
        S.add("dve", lambda e: e.bn_aggr(st[:L, col:col + 2], st[:L, 32:32 + 6 * nchunk].rearrange("p (c s) -> p c s", s=6)), [stB], [stB])

    def rstd_from(st, stB, L, src_col, dst_col, use_mean_col=None):
        if use_mean_col is not None:
            stt("dve", st[:L, src_col:src_col + 1], st[:L, use_mean_col:use_mean_col + 1], st[:L, use_mean_col:use_mean_col + 1],
                st[:L, src_col:src_col + 1], ALU.mult, ALU.add, [stB], [stB])
        act(st[:L, dst_col:dst_col + 1], st[:L, src_col:src_col + 1], AF.Ln, [stB, eps_c_B], [stB], bias=eps_c[:L, :], scale=1.0)
        act(st[:L, dst_col:dst_col + 1], st[:L, dst_col:dst_col + 1], AF.Exp, [stB], [stB], scale=-0.5)

    S.dma("sp", ident_f[:, :], c_ident, writes=[ident_f_B])
    S.dma("pool", ident_b[:, :], c_ident, writes=[ident_b_B])
    S.dma("sp", utri[:, :], c_utri, writes=[utri_B])
    S.dma("sp", utri_s[:, :], c_utri_s, writes=[utri_s_B])
    S.dma("sp", same_s[:, :], c_same_s, writes=[same_s_B])
    S.dma("pool", mask_p[:, :], c_mask_p, writes=[mask_p_B])
    S.dma("pool", mask_s[:, :], c_mask_s, writes=[mask_s_B])
    S.dma("pool", d96[:, :], c_d96, writes=[d96_B])
    S.dma("pool", d96s[:, :], c_d96s, writes=[d96s_B])
    memset("dve", nones_b[:, :], -1.0, [nones_b_B])
    for dz_, dzB_ in dtAz:
        memset("dve", dz_[:, :], 0.0, [dzB_])
    S.dma("pool", seqm[:].rearrange("p a b -> p (a b)"), c_seqm.partition_broadcast(128), writes=[seqm_B])
    S.dma("sp", seqm_t[:, :], c_seqm_t, writes=[seqm_t_B])
    S.dma("sp", sel4[:, :], c_sel4, writes=[sel4_B])
    S.dma("sp", oh16[:, :], c_oh16, writes=[oh16_B])
    S.dma("sp", msel[:, :], c_msel, writes=[msel_B])
    bst_, bst_B = wk[3]
    S.dma("sp", bst_[0:8, 0:128], gm_bs, writes=[bst_B])
    S.dma("sp", bst_[8:16, 0:128], gm_bs, writes=[bst_B])
    S.dma("pool", wdt[:], w_in[:, SPLITS["dt"]:SPLITS["dt"] + 16].rearrange("(k p) c -> p k c", p=128), writes=[wdt_B])
    memset("dve", ones_f[:, :], 1.0, [ones_f_B])
    memset("dve", ones_b[:, :], 1.0, [ones_b_B])
    memset("dve", eps_c[:, :], EPS, [eps_c_B])
    memset("dve", hT[:, :], 0.0, [hT_B])
    memset("dve", convhist[:], 0.0, [convhist_B])
    S.dma("sp", hvec[:, 0, :], dt_bias.partition_broadcast(128), writes=[hvec_B])
    S.dma("sp", hvec[:, 1, :], a_log.partition_broadcast(128), writes=[hvec_B])
    S.dma("sp", hvec[:, 2, :], d_skip.partition_broadcast(128), writes=[hvec_B])
    act(hvec[:, 1, :], hvec[:, 1, :], AF.Exp, [hvec_B], [hvec_B])
    ts1("dve", hvec[:, 1, :], hvec[:, 1, :], -1.0, ALU.mult, [hvec_B], [hvec_B])
    cp("dve", dsk_b[:, :], hvec[:, 2, :], [hvec_B], [dsk_b_B])
    bhi_ = bf(bst_)[0:16, 512:640]
    cp("dve", bhi_, bst_[0:16, 0:128], [bst_B], [bst_B])
    tt("dve", bst_[0:16, 128:256], bst_[0:16, 0:128], bhi_, ALU.subtract, [bst_B], [bst_B])
    ts1("dve", bst_[0:16, 128:256], bst_[0:16, 128:256], msel[:, 1:2], ALU.mult, [bst_B, msel_B], [bst_B])
    stt("dve", bs16[:, :], bhi_, msel[:, 0:1], bst_[0:16, 128:256], ALU.mult, ALU.add, [bst_B, msel_B], [bs16_B])
    cp("dve", bs16s[:].rearrange("h (i a) -> h i a", a=4), bs16[:, 0:4].unsqueeze(1).to_broadcast([16, 16, 4]), [bs16_B], [bs16s_B])

    cst = ring[0][0][:].bitcast(F32)
    cstB = ring[0][1]
    S.dma("sp", cst[0:1, 0:1536], conv_b, writes=[cstB])
    S.dma("sp", cst[1:5, 0:1536], conv_w, writes=[cstB])
    pv, pB = bank()
    for c in range(12):
        tr(pv[:, c * 8:c * 8 + 5], cst[0:5, c * 128:(c + 1) * 128], ident_f[0:5, 0:5], [cstB, ident_f_B], [pB])
    cp("dve", cwc[:, :, 0:5], pv[:, 0:96].rearrange("p (c k) -> p c k", k=8)[:, :, 0:5], [pB], [cwc_B])
    sst = ring[1][0][:].bitcast(F32)
    sstB = ring[1][1]
    S.dma("sp", sst[0:48, 0:1536], st_conv, writes=[sstB])
    p2, p2B = bank2()
    for c in range(12):
        tr(p2[:, c * 64:c * 64 + 48], sst[0:48, c * 128:(c + 1) * 128], ident_f[0:48, 0:48], [sstB, ident_f_B], [p2B[c // 8]])
    cp("dve", scT[:, 0:8, :], p2[:, 0:512].rearrange("p (c k) -> p c k", k=64)[:, :, 0:48], [p2B[0]], [scT_B])
    cp("dve", scT[:, 8:12, :], p2[:, 512:768].rearrange("p (c k) -> p c k", k=64)[:, :, 0:48], [p2B[1]], [scT_B])

    g0, g0B = wk[0]
    g1, g1B = wk[1]
    g2, g2B = wk[2]
    S.dma("sp", g0[:].rearrange("p (h s) -> p h s", h=8), gm_w.rearrange("h t s -> t h s"), writes=[g0B])
    p2, p2B = bank2()
    for h in range(8):
        tr(p2[:, h * 128:(h + 1) * 128], g0[:, h * 128:(h + 1) * 128], ident_f[:, :], [g0B, ident_f_B], [p2B[h // 4]])
    for hb in range(2):
        tt("dve", g1[:, hb * 512:(hb + 1) * 512].rearrange("p (h t) -> p h t", h=4),
           p2[:, hb * 512:(hb + 1) * 512].rearrange("p (h t) -> p h t", h=4),
           utri[:, :].unsqueeze(1).to_broadcast([128, 4, 128]), ALU.mult, [p2B[hb], utri_B], [g1B])
    cp("dve", WT[:].rearrange("p h t -> p (h t)"), g1[:, :], [g1B], [WT_B])
    cp("dve", g2[0:4, 0:512].rearrange("p (h j b) -> p h j b", h=8, j=16),
       g1[0:4, :].rearrange("p (h t) -> p h t", h=8)[:, :, 0:4].unsqueeze(2).to_broadcast([4, 8, 16, 4]), [g1B], [g2B])
    pv, pB = bank()
    mm(pv[0:64, 0:512], sel4[0:4, 0:64], g2[0:4, 0:512], True, True, [sel4_B, g2B], [pB])
    tt("dve", BD[:], pv[0:64, 0:512].rearrange("p (h t) -> p h t", h=8),
       same_s[:, :].unsqueeze(1).to_broadcast([64, 8, 64]), ALU.mult, [pB, same_s_B], [BD_B])

    if stop == "setup":
        return nc, S
    def token_norm_to_T(x_tile, xB, L, gbc, gbcB, xs_bf, xsB, st, stB, dstT, dstB, c0):
        mean_var(x_tile[:L, :], L, 1024, xB, st, stB, 0)
        rstd_from(st, stB, L, 1, 2, use_mean_col=0)
        stt("dve", xs_bf[:L, 0:1024], x_tile[:L, :], st[:L, 2:3], gbc[:L, :], ALU.mult, ALU.mult, [xB, stB, gbcB], [xsB])
        pvb, pB_ = bank_bf()
        for k in range(8):
            tr(pvb[:, k * L:(k + 1) * L], xs_bf[:L, k * 128:(k + 1) * 128], ident_b[:L, :L], [xsB, ident_b_B], [pB_])
        cp("act", dstT[:, :, c0:c0 + L], pvb[:, 0:8 * L].rearrange("p (k l) -> p k l", k=8), [pB_], [dstB])

    def phase_M():
        mg_bc, mg_bcB = load_bc(mem_norm_g)
        memnT = bf(wk[3][0])[:, 0:2048].rearrange("p (k m) -> p k m", k=8)
        memnT_B = wk[3][1]
        for i in range(2):
            xt, xB = wk[i]
            S.dma("sp", xt[:, :], mem[i * 128:(i + 1) * 128, :], writes=[xB])
            token_norm_to_T(xt, xB, 128, mg_bc, mg_bcB, bf(wk[2][0])[:, i * 1024:(i + 1) * 1024], wk[2][1], sm[0][0], sm[0][1],
                            memnT, memnT_B, i * 128)
        for which, wsrc, odst in ((0, w_mk, o_mk), (1, w_mv, o_mv)):
            for half in range(2):
                slot, slotB = load_w(wsrc[:, half * 512:(half + 1) * 512], 8, 512)
                for mt in range(2):
                    stg, stgB = wk[4 + 2 * which + mt]
                    pv, pB = bank()
                    for k in range(8):
                        mm(pv[:, 0:512], memnT[:, k, mt * 128:(mt + 1) * 128], slot[:, k, :], k == 0, k == 7, [memnT_B, slotB], [pB])
                    cp("act", stg[:, half * 512:(half + 1) * 512], pv[:, 0:512], [pB], [stgB])
                    if which == 1:
                        cp("dve", v_tm[:, mt, half * 512:(half + 1) * 512], pv[:, 0:512], [pB], [v_tm_B])
                if which == 0:
                    for jp in range(2):
                        pv, pB = bank()
                        for j2 in range(2):
                            jj = jp * 2 + j2
                            for k in range(8):
                                mm(pv[:, j2 * 256:(j2 + 1) * 256], slot[:, k, jj * 128:(jj + 1) * 128], memnT[:, k, :], k == 0, k == 7,
                                   [memnT_B, slotB], [pB])
                        cp("act", kT[:, half * 4 + jp * 2:half * 4 + jp * 2 + 2, :], pv[:, 0:512].rearrange("p (j m) -> p j m", j=2), [pB], [kT_B])
            for mt in range(2):
                stg, stgB = wk[4 + 2 * which + mt]
                S.dma("sp", odst[mt * 128:(mt + 1) * 128, :], stg[:, :], reads=[stgB])


    if stop == "M":
        return nc, S
    NPASS = 4
    PS_A = 1
    PS_C = 1
    PS_B = 2
    PS_D = 2
    for p in range(NPASS):
        tiles_pr = [(t * 128, 128, False) for t in range(4)]
        tiles = tiles_pr + ([(512, 64, True)] if p == PS_A else [])
        tilesB = tiles_pr + ([(512, 64, True)] if p == PS_B else [])
        tilesD = tiles_pr + ([(512, 64, True)] if p == PS_D else [])
        NT = 576 if p == PS_A else 512

        mark("P0%d" % p)
        ng_bc, ng_bcB = load_bc(norm_g)
        for i, (c0, L, smp) in enumerate(tiles):
            xt, xB = wk[i % 2]
            src = x_s[:, :] if smp else x_p[p * 512 + c0:p * 512 + c0 + 128, :]
            S.dma("sp", xt[:L, :], src, writes=[xB])
            token_norm_to_T(xt, xB, L, ng_bc, ng_bcB, bf(wk[2 + i % 2][0]), wk[2 + i % 2][1], sm[(0, 7)[i % 2]][0], sm[(0, 7)[i % 2]][1], hnT, hnTB[i], c0)

        if stop == "P0":
            return nc, S
        mark("A%d" % p)
        gg_bc, gg_bcB = load_bc(gm_norm_g)
        gb_bc, gb_bcB = load_bc(gm_norm_b)
        vb = []
        for i in range(len(tiles)):
            t_, b_ = wk[5 + i // 2]
            vb.append((bf(t_)[:, (i % 2) * 1024:(i % 2) * 1024 + 1024], b_))
        for cb in range(2):
            slot, slotB = w_in_block(SPLITS["v"] + cb * 512)
            for i, (c0, L, smp) in enumerate(tiles):
                pv, pB = bank()
                for k in range(8):
                    mm(pv[:L, 0:512], hnT[:, k, c0:c0 + L], slot[:, k, :], k == 0, k == 7, [hnTB[i], slotB], [pB])
                act(wk[i][0][:L, cb * 512:(cb + 1) * 512], pv[:L, 0:512], AF.Gelu, [pB], [wk[i][1]])
        for i, (c0, L, smp) in enumerate(tiles):
            vg, vgB = wk[i]
            st, stB = sm[(0, 7)[i % 2]]
            mean_var(vg[:L, :], L, 1024, vgB, st, stB, 0)
            rstd_from(st, stB, L, 1, 2)
            ts("dve", vg[:L, :], vg[:L, :], st[:L, 0:1], st[:L, 2:3], ALU.subtract, ALU.mult, [vgB, stB], [vgB])
            tt("dve", vg[:L, :], vg[:L, :], gg_bc[:L, :], ALU.mult, [vgB, gg_bcB], [vgB])
            if smp:
                tt("dve", vg[:L, :], vg[:L, :], gb_bc[:L, :], ALU.add, [vgB, gb_bcB], [vgB])
                S.dma("sp", o_gv, vg[:L, :], reads=[vgB])
                cp("act", vb[i][0][:L, :], vg[:L, :], [vgB], [vb[i][1]])
            else:
                tt("dve", vb[i][0][:L, :], vg[:L, :], gb_bc[:L, :], ALU.add, [vgB, gb_bcB], [vb[i][1]])
        bsr, bsrB = wk[10]
        bsrs, bsrsB = wk[13]
        bsrBs = [Buf() for _ in range(8)]
        bsrsBs = [Buf() for _ in range(8)]
        WS_IDS = (8, 9, 11, 12, 14, 15, 16, 17)
        wsbufs = [[Buf() for _ in range(3)] for _ in range(8)]
        a_par = [wk[k_][1] for k_ in WS_IDS] + [wk[10][1], wk[13][1]]
        a_sub = [b_ for l_ in wsbufs for b_ in l_] + bsrBs + bsrsBs
        fence(a_par + a_sub)

        def a_proj(slt, sltB, jj, dst, func, wsB):
            pv, pB = bank()
            for k in range(8):
                mm(pv[:, 0:512], slt[:, k, jj * 128:(jj + 1) * 128], hnT[:, k, 0:512], k == 0, k == 7, hnTB[0:4] + [sltB], [pB])
            act(dst[:, 0:512], pv[:, 0:512], func, [pB], [wsB])
            if p == PS_A:
                pv, pB = bank()
                for k in range(8):
                    mm(pv[:, 0:64], slt[:, k, jj * 128:(jj + 1) * 128], hnT[:, k, 512:576], k == 0, k == 7, [hnTB[4], sltB], [pB])
                act(dst[:, 512:576], pv[:, 0:64], func, [pB], [wsB])

        for j in range(8):
            jj = j % 4
            if jj == 0:
                slot_u, slot_uB = w_in_block(SPLITS["u"] + (j // 4) * 512)
            wsb = bf(wk[WS_IDS[j]][0])
            a_proj(slot_u, slot_uB, jj, wsb[:, 0:576], AF.Gelu, wsbufs[j][0])
        for j in range(8):
            jj = j % 4
            if jj == 0:
                slot_g, slot_gB = w_in_block(SPLITS["gate"] + (j // 4) * 512)
            wsb = bf(wk[WS_IDS[j]][0])
            uaB_, sgB_, usgB_ = wsbufs[j]
            ua = wsb[:, 0:576]
            sg = wsb[:, 576:1152]
            usg = wsb[:, 1152:1728]
            a_proj(slot_g, slot_gB, jj, sg, AF.Silu, sgB_)
            tt("dve", usg[:, 0:NT], ua[:, 0:NT], sg[:, 0:NT], ALU.mult, [uaB_, sgB_], [usgB_])
            bsr16 = bf(bsr)[0:16, j * 128:(j + 1) * 128]
            ts1("dve", bsr16, bs16[:, :], oh16[:, j:j + 1], ALU.mult, [bs16_B, oh16_B], [bsrBs[j]])
            pv, pB = bank()
            for i in range(4):
                mm(pv[:, i * 128:(i + 1) * 128], vb[i][0][:, j * 128:(j + 1) * 128], WT[:, j, :], True, False, [vb[i][1], WT_B], [pB])
                mm(pv[:, i * 128:(i + 1) * 128], ones_b[0:16, 0:128], bsr16, False, True, [ones_b_B, bsrBs[j]], [pB])
            tt("dve", mixT[:, j, 0:512], pv[:, 0:512], usg[:, 0:512], ALU.mult, [pB, usgB_], [mxB[j][0]])
            if p == PS_A:
                bsrs16 = bf(bsrs)[0:16, j * 64:(j + 1) * 64]
                ts1("dve", bsrs16, bs16s[:, :], oh16[:, j:j + 1], ALU.mult, [bs16s_B, oh16_B], [bsrsBs[j]])
                pv, pB = bank()
                mm(pv[:, 0:64], vb[4][0][0:64, j * 128:(j + 1) * 128], BD[:, j, :], True, False, [vb[4][1], BD_B], [pB])
                mm(pv[:, 0:64], ones_b[0:16, 0:128], bsrs16, False, True, [ones_b_B, bsrsBs[j]], [pB])
                tt("dve", mixT[:, j, 512:576], pv[:, 0:64], usg[:, 512:576], ALU.mult, [pB, usgB_], [mxB[j][1]])

        if stop == "A":
            return nc, S
        mark("B%d" % p)
        fence(a_par + a_sub)
        sg_bc, sg_bcB = load_bc(ssm_norm_g)
        slot_z = [w_in_block(SPLITS["z"] + zb * 512) for zb in range(2)]

        def xact(c):
            t_, b_ = wk[c // 3]
            return bf(t_)[:, (c % 3) * 576:(c % 3) * 576 + 576], b_

        craw, crawB = wk[6]
        for xb in range(3):
            slot, slotB = w_in_block(SPLITS["xbc"] + xb * 512)
            for cc in range(4):
                c = xb * 4 + cc
                xraw, xrawB = wk[(4, 7, 9)[c % 3]]
                acc, accB = wk[(5, 8, 10)[c % 3]]
                xa, xaB = xact(c)
                pv, pB = bank()
                for k in range(8):
                    mm(pv[:, 0:512], slot[:, k, cc * 128:(cc + 1) * 128], hnT[:, k, 0:512], k == 0, k == 7, hnTB[0:4] + [slotB], [pB])
                cp("dve", xraw[:, 0:3], convhist[:, c, :], [convhist_B], [xrawB])
                cp("act", xraw[:, 3:515], pv[:, 0:512], [pB], [xrawB])
                cp("dve", convhist[:, c, :], xraw[:, 512:515], [xrawB], [convhist_B])
                ts("dve", acc[:, 0:512], xraw[:, 0:512], cwc[:, c, 1:2], cwc[:, c, 0:1], ALU.mult, ALU.add, [xrawB, cwc_B], [accB])
                for kk in range(1, 4):
                    stt("dve", acc[:, 0:512], xraw[:, kk:kk + 512], cwc[:, c, 1 + kk:2 + kk], acc[:, 0:512], ALU.mult, ALU.add,
                        [xrawB, cwc_B, accB], [accB])
                act(xa[:, 0:512], acc[:, 0:512], AF.Silu, [accB], [xaB])
                if p == PS_B:
                    pv, pB = bank()
                    for k in range(8):
                        mm(pv[:, 0:64], slot[:, k, cc * 128:(cc + 1) * 128], hnT[:, k, 512:576], k == 0, k == 7, [hnTB[4], slotB], [pB])
                    xrs = xraw[:, 520:632].rearrange("p (i k) -> p i k", k=7)
                    cp("dve", xrs[:, :, 0:3], scT[:, c, :].rearrange("p (i k) -> p i k", k=3), [scT_B], [xrawB])
                    cp("act", xrs[:, :, 3:7], pv[:, 0:64].rearrange("p (i k) -> p i k", k=4), [pB], [xrawB])
                    accs = acc[:, 512:576].rearrange("p (i k) -> p i k", k=4)
                    ts("dve", accs, xrs[:, :, 0:4], cwc[:, c, 1:2], cwc[:, c, 0:1], ALU.mult, ALU.add, [xrawB, cwc_B], [accB])
                    for kk in range(1, 4):
                        stt("dve", accs, xrs[:, :, kk:kk + 4], cwc[:, c, 1 + kk:2 + kk], accs, ALU.mult, ALU.add,
                            [xrawB, cwc_B, accB], [accB])
                    act(xa[:, 512:576], acc[:, 512:576], AF.Silu, [accB], [xaB])
            if p == NPASS - 1:
                pv, pB = bank()
                for k in range(8):
                    mm(pv[:, 0:512], hnT[:, k, 384:512], slot[:, k, :], k == 0, k == 7, [hnTB[3], slotB], [pB])
                cp("act", craw[:, 0:512], pv[:, 0:512], [pB], [crawB])
                S.dma("sp", o_conv_p[:, xb * 512:(xb + 1) * 512], craw[125:128, 0:512], reads=[crawB])
            if p == PS_B:
                pv, pB = bank()
                for k in range(8):
                    mm(pv[0:64, 0:512], hnT[:, k, 512:576], slot[:, k, :], k == 0, k == 7, [hnTB[4], slotB], [pB])
                cp("act", craw[0:64, 512:1024], pv[0:64, 0:512], [pB], [crawB])
                ocs = o_conv_s.rearrange("(i k) c -> k i c", k=3)
                for kk in range(3):
                    S.dma("sp", ocs[kk, :, xb * 512:(xb + 1) * 512], craw[1 + kk:64:4, 512:1024], reads=[crawB])

        mark("Bt%d" % p)
        def ssd_tile(i, c0, L, smp):
            ws_ = [4, 5, 6, 7, 8, 9, 10, 11, 12] if i % 2 == 0 else [13, 14, 15, 16, 17, 18, 19, 20, 21]
            ss_ = [1, 2, 3, 4] if i % 2 == 0 else [8, 9, 10, 11]
            ba_ = 0 if i % 2 == 0 else 2
            bctr_ = [0]
            b2ctr_ = [0]

            def bank():
                k_ = ba_ + bctr_[0] % 2
                bctr_[0] += 1
                return pp[k_ // 2][:, (k_ % 2) * 512:(k_ % 2) * 512 + 512], PB[k_]

            def bank_bf():
                k_ = ba_ + bctr_[0] % 2
                bctr_[0] += 1
                return ppb[k_ // 2][:, (k_ % 2) * 1024:(k_ % 2) * 1024 + 1024], PB[k_]

            def bank2(kind=None):
                if kind == "py":
                    q_ = 2
                elif kind == "po":
                    q_ = 3
                elif kind == "own":
                    q_ = ba_ // 2
                else:
                    q_ = b2ctr_[0] % 2
                    b2ctr_[0] += 1
                return pp[q_], [PB[2 * q_], PB[2 * q_ + 1]]
            Um, UmB = (utri_s, utri_s_B) if smp else (utri, utri_B)
            On, OnB = (same_s, same_s_B) if smp else (ones_f, ones_f_B)
            dl, dlB = (d96s, d96s_B) if smp else (d96, d96_B)
            mk_, mkB = (mask_s, mask_s_B) if smp else (mask_p, mask_p_B)
            t6, t6B = wk[ws_[2]]
            xs_tm = bf(t6)[:, 0:1024]
            bm_tm = bf(t6)[:, 1024:1280]
            t7, t7B = wk[ws_[3]]
            xdt = bf(t7)[:, 0:1024]
            xdtE = bf(t7)[:, 1024:2048]
            t8, t8B = wk[ws_[4]]
            xsD = bf(t8)[:, 0:1024]
            cbTs = bf(t8)[:, 1024:1280]
            t9, t9B = wk[ws_[5]]
            hTb = bf(t9)[:, 0:1024]
            hbB = hTb_bufs[i % 2]
            ybB = yb_bufs[i % 2]
            yb = bf(t9)[:, 1024:2048]
            t10, t10B = wk[ws_[6]]
            G = bf(t10)
            zs, zsB = wk[ws_[7]]
            yy, yyB = wk[ws_[8]]
            acs, acsB = wk[ws_[0]]
            r1t, r1B = wk[ws_[1]]
            s1, s1B = sm[ss_[0]]
            s2, s2B = sm[ss_[1]]
            s3, s3B = sm[ss_[2]]
            s4, s4B = sm[ss_[3]]
            pvb, pB = bank_bf()
            for c in range(8):
                xa, xaB = xact(c)
                tr(pvb[:L, c * 128:(c + 1) * 128], xa[:, c0:c0 + L], ident_b[:, :], [xaB, ident_b_B], [pB])
            cp("act", xs_tm[:L, :], pvb[:L, 0:1024], [pB], [t6B])
            pvb, pB = bank_bf()
            for c in range(2):
                xa, xaB = xact(8 + c)
                tr(pvb[:L, c * 128:(c + 1) * 128], xa[:, c0:c0 + L], ident_b[:, :], [xaB, ident_b_B], [pB])
            cp("act", bm_tm[:L, :], pvb[:L, 0:256], [pB], [t6B])
            yield
            pv, pB = bank()
            for k in range(8):
                mm(pv[:L, 0:16], hnT[:, k, c0:c0 + L], wdt[:, k, :], k == 0, k == 7, [hnTB[i], wdt_B], [pB])
            tt("dve", s1[:L, 0:16], pv[:L, 0:16], hvec[:L, 0, :], ALU.add, [pB, hvec_B], [s1B])
            stt("dve", s1[:L, 16:32], s1[:L, 0:16], -1.0, s1[:L, 0:16], ALU.mult, ALU.max, [s1B], [s1B])
            act(s1[:L, 16:32], s1[:L, 16:32], AF.Exp, [s1B], [s1B], scale=-1.0)
            act(s1[:L, 16:32], s1[:L, 16:32], AF.Ln, [s1B], [s1B], bias=1.0)
            stt("dve", s1[:L, 0:16], s1[:L, 0:16], 0.0, s1[:L, 16:32], ALU.max, ALU.add, [s1B], [s1B])
            tt("dve", s1[:L, 32:48], s1[:L, 0:16], hvec[:L, 1, :], ALU.mult, [s1B, hvec_B], [s1B])
            yield
            pc0, pc0B = bank()
            dz, dzB = dtAz[i % 2]
            ts1("dve", dz[:L, :].rearrange("p (g b k) -> p g b k", g=4, b=3)[:, :, :, 0:4],
                s1[:L, 32:48].rearrange("p (g k) -> p g k", g=4).unsqueeze(2).to_broadcast([L, 4, 3, 4]), -1.0, ALU.mult, [s1B], [dzB])
            for g in range(4):
                mm(pc0[0:96, g * 128:g * 128 + L], dz[:L, g * 96:(g + 1) * 96], Um[:L, :L], True, True, [dzB, UmB], [pc0B])
            pc1, pc1B = bank()
            mm(pc1[:L, 0:16], Um[:L, :L], s1[:L, 32:48], True, True, [s1B, UmB], [pc1B])
            mm(pc1[:L, 16:32], On[:L, :L], s1[:L, 32:48], True, True, [s1B, OnB], [pc1B])
            cp("act", s2[:L, 0:32], pc1[:L, 0:32], [pc1B], [s2B])
            tt("dve", s3[:L, 32:48], s2[:L, 16:32], s2[:L, 0:16], ALU.subtract, [s2B], [s3B])
            act(s3[:L, 32:48], s3[:L, 32:48], AF.Exp, [s3B], [s3B])
            act(s3[:L, 0:32], s2[:L, 0:32], AF.Exp, [s2B], [s3B])
            tt("dve", s3[:L, 48:64], s1[:L, 0:16], s3[:L, 32:48], ALU.mult, [s1B, s3B], [s3B])
            def v3(ap):
                return ap.rearrange("p (g l) -> p g l", g=4)[:, :, 0:L]
            Tb = bf(r1t)
            r1hB = r1h_bufs[i % 2]
            cp("act", v3(acs[0:96, 0:512]), v3(pc0[0:96, 0:512]), [pc0B], [acsB])
            cp("dve", v3(Tb[0:96, 0:512]), v3(acs[0:96, 0:512]), [acsB], [r1B])
            for q0 in (32, 64):
                tt("dve", v3(acs[q0:q0 + 32, 512:1024]), v3(acs[q0:q0 + 32, 0:512]), v3(Tb[q0:q0 + 32, 0:512]), ALU.subtract, [acsB, r1B], [acsB])
            for q0 in (32, 64):
                cp("dve", v3(Tb[q0:q0 + 32, 0:512]), v3(acs[q0:q0 + 32, 512:1024]), [acsB], [r1B])
            tt("dve", v3(acs[64:96, 512:1024]), v3(acs[64:96, 512:1024]), v3(Tb[64:96, 0:512]), ALU.subtract, [acsB, r1B], [acsB])
            cp("dve", v3(Tb[64:96, 0:512]), v3(acs[64:96, 512:1024]), [acsB], [r1B])
            yield
            for g in range(4):
                r1 = Tb[0:96, 1024 + (g % 2) * 512:1024 + (g % 2) * 512 + 4 * L]
                tt("pool", r1.rearrange("p (k l) -> p k l", k=4), dl[0:96, 0:4 * L].rearrange("p (k l) -> p k l", k=4),
                   Tb[0:96, g * 128:g * 128 + L].unsqueeze(1).to_broadcast([96, 4, L]), ALU.mult, [dlB, r1B], [r1hB[g % 2]])
                pd, pdB = bank()
                mm(pd[:L, 0:4 * L], nones_b[0:96, 0:L], r1, True, False, [nones_b_B, r1hB[g % 2], r1B], [pdB])
                mm(pd[:L, 0:4 * L], Tb[0:96, g * 128:g * 128 + L], dl[0:96, 0:4 * L], False, False, [r1B, dlB], [pdB])
                mm(pd[:L, 0:4 * L], ident_b[:L, :L], mk_[:L, 0:4 * L], False, True, [ident_b_B, mkB], [pdB])
                act(G[:L, g * 4 * L:(g + 1) * 4 * L], pd[:L, 0:4 * L], AF.Exp, [pdB], [t10B])
            yield
            pv, pB = bank()
            for g2 in range(2):
                bT, bTB = xact(8 + g2)
                cT, cTB = xact(10 + g2)
                mm(pv[:L, g2 * L:(g2 + 1) * L], bT[:, c0:c0 + L], cT[:, c0:c0 + L], True, True, [bTB, cTB], [pB])
            cp("act", cbTs[:L, 0:2 * L], pv[:L, 0:2 * L], [pB], [t8B])
            yield
            for g2 in range(2):
                Gv = G[:L, g2 * 8 * L:(g2 + 1) * 8 * L].rearrange("p (h l) -> p h l", h=8)
                tt("dve", Gv, Gv, cbTs[:L, g2 * L:(g2 + 1) * L].unsqueeze(1).to_broadcast([L, 8, L]), ALU.mult, [t10B, t8B], [t10B])
            yield
            xs3 = xs_tm[:L, :].rearrange("p (h q) -> p h q", h=16)
            tt("dve", xdt[:L, :].rearrange("p (h q) -> p h q", h=16), xs3, s1[:L, 0:16].unsqueeze(2).to_broadcast([L, 16, 64]),
               ALU.mult, [t6B, s1B], [t7B])
            tt("dve", xdtE[:L, :].rearrange("p (h q) -> p h q", h=16), xs3, s3[:L, 48:64].unsqueeze(2).to_broadcast([L, 16, 64]),
               ALU.mult, [t6B, s3B], [t7B])
            tt("dve", xsD[:L, :].rearrange("p (h q) -> p h q", h=16), xs3, dsk_b[:L, :].unsqueeze(2).to_broadcast([L, 16, 64]),
               ALU.mult, [t6B, dsk_b_B], [t8B])
            yield
            for zb in range(2):
                slt, sltB = slot_z[zb]
                pv, pB = bank()
                for k in range(8):
                    mm(pv[:L, 0:512], hnT[:, k, c0:c0 + L], slt[:, k, :], k == 0, k == 7, [hnTB[i], sltB], [pB])
                act(zs[:L, zb * 512:(zb + 1) * 512], pv[:L, 0:512], AF.Silu, [pB], [zsB])
            if not smp:
                cp("act", hTb[:, :], hT[:, :], [hT_B], [hbB])
                psn, psnB = bank2("own")
                for g2 in range(2):
                    mm(psn[:, g2 * 512:(g2 + 1) * 512], bm_tm[:L, g2 * 128:(g2 + 1) * 128], xdtE[:L, g2 * 512:(g2 + 1) * 512], True, True,
                       [t6B, t7B], [psnB[g2]])
                tt("dve", hT[:].rearrange("p (h q) -> p h q", h=16), hT[:].rearrange("p (h q) -> p h q", h=16),
                   s3[:, 16:32].unsqueeze(2).to_broadcast([128, 16, 64]), ALU.mult, [hT_B, s3B], [hT_B])
                tt("dve", hT[:, :], hT[:, :], psn[:, :], ALU.add, [hT_B] + psnB, [hT_B])
            yield "B"
            yield
            py, pyB = bank2("py")
            pin(pyB)
            for g2 in range(2):
                mm(py[:L, g2 * 512:(g2 + 1) * 512], ident_b[:L, :L], xsD[:L, g2 * 512:(g2 + 1) * 512], True, True,
                   [ident_b_B, t8B], [pyB[g2]], skip=True)
            for h in range(16):
                mm(py[:L, h * 64:(h + 1) * 64], G[:L, h * L:(h + 1) * L], xdt[:L, h * 64:(h + 1) * 64], False, True,
                   [t10B, t7B], [pyB[h // 8]], skip=True)
            yield
            po, poB = bank2("po")
            pin(poB)
            if not smp:
                for g2 in range(2):
                    cT, cTB = xact(10 + g2)
                    mm(po[:L, g2 * 512:(g2 + 1) * 512], cT[:, c0:c0 + L], hTb[:, g2 * 512:(g2 + 1) * 512], True, True,
                       [cTB, hbB], [poB[g2]])
            else:
                fence([wk[18][1], hTb_bufs[1], yb_bufs[1]])
                cmm, cmmB = wk[13]
                cmmv = bf(cmm).rearrange("p (g i t) -> p g i t", g=2, i=16)
                for g2 in range(2):
                    cT, cTB = xact(10 + g2)
                    tt("dve", cmmv[:, g2], seqm[:], cT[:, 512:576].unsqueeze(1).to_broadcast([128, 16, 64]), ALU.mult,
                       [seqm_B, cTB], [cmmB])
                bmmv = []
                for hf in range(2):
                    bt, btB = wk[14 + hf]
                    v_ = bf(bt)[0:64, :].rearrange("p (i c) -> p i c", i=8)
                    tt("dve", v_, bm_tm[0:64, 0:256].unsqueeze(1).to_broadcast([64, 8, 256]),
                       seqm_t[:, hf * 8:(hf + 1) * 8].unsqueeze(2).to_broadcast([64, 8, 256]), ALU.mult, [t6B, seqm_t_B], [btB])
                    bmmv.append((v_, btB))
                x18, x18B = wk[18]
                h0T = bf(x18)[:, 0:1024]
                pdp, pdpB = bank()
                for hh in range(2):
                    rh = x18[0:64, 512 + hh * 128:512 + (hh + 1) * 128]
                    tt("dve", rh.rearrange("p (i a) -> p i a", i=16), seqm_t[:, :].unsqueeze(2).to_broadcast([64, 16, 8]),
                       s1[0:64, 32:48].rearrange("p (a h) -> p h a", h=2)[:, hh, :].unsqueeze(1).to_broadcast([64, 16, 8]),
                       ALU.mult, [seqm_t_B, s1B], [x18B])
                    mm(pdp[hh * 64:(hh + 1) * 64, 0:128], ones_f[0:64, 0:64], rh, True, True, [ones_f_B, x18B], [pdpB])
                decP = x18[:, 768:896]
                act(decP, pdp[:, 0:128], AF.Exp, [pdpB], [x18B])
                h0s_ = [wk[16], wk[19]]
                hns_ = [wk[17], wk[20]]
                h0Ts_ = [(h0T, x18B), (bf(wk[21][0])[:, 0:1024], wk[21][1])]
                h0b_ = bf(wk[21][0])[:, 1024:2048]
                h0bB_ = wk[21][1]
                for sq in range(16):
                    h0, h0B = h0s_[sq % 2]
                    hn, hnB = hns_[sq % 2]
                    h0Tq, h0TqB = h0Ts_[sq % 2]
                    S.dma("sp", h0[:].rearrange("p (a n) -> p a n", a=8), st_ssm[sq].rearrange("a p n -> p a n"), writes=[h0B])
                    cp("act", h0b_[:, :], h0[:, :], [h0B], [h0bB_])
                    p2b_, p2bB_ = bank_bf()
                    for a in range(8):
                        tr(p2b_[:, a * 128:(a + 1) * 128], h0b_[:, a * 128:(a + 1) * 128], ident_b[:, :], [h0bB_, ident_b_B], [p2bB_])
                    cp("dve", h0Tq[:, :], p2b_[:, 0:1024], [p2bB_], [h0TqB])
                    for g2 in range(2):
                        mm(po[0:64, g2 * 512:(g2 + 1) * 512], cmmv[:, g2, sq, :], h0Tq[:, g2 * 512:(g2 + 1) * 512], sq == 0, sq == 15,
                           [cmmB, h0TqB], [poB[g2]])
                    ps2, ps2B = bank2()
                    bv, bvB = bmmv[sq // 8]
                    for a in range(8):
                        mm(ps2[:, a * 128:(a + 1) * 128], xdtE[0:64, a * 128:(a + 1) * 128], bv[:, sq % 8, (a // 4) * 128:(a // 4) * 128 + 128],
                           True, True, [t7B, bvB], [ps2B[a // 4]])
                    for a in range(8):
                        stt("dve", hn[:, a * 128:(a + 1) * 128], h0[:, a * 128:(a + 1) * 128], decP[:, sq * 8 + a:sq * 8 + a + 1],
                            ps2[:, a * 128:(a + 1) * 128], ALU.mult, ALU.add, [h0B, x18B, ps2B[a // 4]], [hnB])
                    S.dma("sp", o_ssm_s[sq].rearrange("a p n -> p a n"), hn[:].rearrange("p (a n) -> p a n", a=8), reads=[hnB])
            yield
            tt("dve", yy[:L, :].rearrange("p (h q) -> p h q", h=16), po[:L, :].rearrange("p (h q) -> p h q", h=16),
               s3[:L, 0:16].unsqueeze(2).to_broadcast([L, 16, 64]), ALU.mult, poB + [s3B], [yyB])
            tt("dve", yy[:L, :], py[:L, :], yy[:L, :], ALU.add, pyB + [yyB], [yyB])
            unpin(pyB)
            unpin(poB)
            yield
            tt("dve", yy[:L, :], yy[:L, :], zs[:L, :], ALU.mult, [yyB, zsB], [yyB])
            yield
            for g2 in range(2):
                mean_var(yy[:L, g2 * 512:(g2 + 1) * 512], L, 512, yyB, s4, s4B, 2 * g2)
                stt("dve", s4[:L, 2 * g2 + 1:2 * g2 + 2], s4[:L, 2 * g2:2 * g2 + 1], s4[:L, 2 * g2:2 * g2 + 1], s4[:L, 2 * g2 + 1:2 * g2 + 2],
                    ALU.mult, ALU.add, [s4B], [s4B])
                act(s4[:L, 8 + g2:9 + g2], s4[:L, 2 * g2 + 1:2 * g2 + 2], AF.Ln, [s4B, eps_c_B], [s4B], bias=eps_c[:L, :])
                act(s4[:L, 8 + g2:9 + g2], s4[:L, 8 + g2:9 + g2], AF.Exp, [s4B], [s4B], scale=-0.5)
                stt("dve", yb[:L, g2 * 512:(g2 + 1) * 512], yy[:L, g2 * 512:(g2 + 1) * 512], s4[:L, 8 + g2:9 + g2],
                    sg_bc[:L, g2 * 512:(g2 + 1) * 512], ALU.mult, ALU.mult, [yyB, s4B, sg_bcB], [ybB])
            yield
            pvb, pB = ppb[3][:, 0:1024], PB[6]
            for c in range(8):
                tr(pvb[:, c * L:(c + 1) * L], yb[:L, c * 128:(c + 1) * 128], ident_b[:L, :L], [ybB, ident_b_B], [pB])
            cp("act", mixT[:, 8:16, c0:c0 + L], pvb[:, 0:8 * L].rearrange("p (k l) -> p k l", k=8), [pB], [mxB[k_][1 if smp else 0] for k_ in range(8, 16)])
            yield

        t9_all = [wk[9][1], wk[18][1]] + hTb_bufs + yb_bufs
        fence(t9_all)
        gens = [ssd_tile(i_, *t_) for i_, t_ in enumerate(tilesB)]
        prevg = None
        for g_ in gens:
            if prevg is None:
                for v_ in g_:
                    if v_ == "B":
                        break
            else:
                a_done = b_done = False
                while not (a_done and b_done):
                    if not b_done:
                        try:
                            next(prevg)
                        except StopIteration:
                            b_done = True
                    if not a_done:
                        try:
                            if next(g_) == "B":
                                a_done = True
                        except StopIteration:
                            a_done = True
            prevg = g_
        for v_ in prevg:
            pass
        fence(t9_all)
        if p == NPASS - 1:
            p2, p2B = bank2()
            for a in range(8):
                tr(p2[:, a * 128:(a + 1) * 128], hT[:, a * 128:(a + 1) * 128], ident_f[:, :], [hT_B, ident_f_B], [p2B[a // 4]])
            so, soB = wk[12]
            for hb in range(2):
                cp("act", so[:, hb * 512:(hb + 1) * 512], p2[:, hb * 512:(hb + 1) * 512], [p2B[hb]], [soB])
            S.dma("sp", o_ssm_p.rearrange("a p n -> p a n"), so[:].rearrange("p (a n) -> p a n", a=8), reads=[soB])

        if stop == "B":
            return nc, S
        mark("C%d" % p)
        if p == 0:
            phase_M()
        def qc(c):
            t_, b_ = wk[c // 3]
            return bf(t_)[:, (c % 3) * 576:(c % 3) * 576 + 576], b_

        def gc(c):
            t_, b_ = wk[3 + c // 3]
            return bf(t_)[:, (c % 3) * 576:(c % 3) * 576 + 576], b_

        for kind in range(2):
            for blk in range(2):
                slot, slotB = w_in_block((SPLITS["q"] if kind == 0 else SPLITS["mg"]) + blk * 512)
                for cc in range(4):
                    c = blk * 4 + cc
                    dst, dstB = qc(c) if kind == 0 else gc(c)
                    pv, pB = bank()
                    for k in range(8):
                        mm(pv[:, 0:512], slot[:, k, cc * 128:(cc + 1) * 128], hnT[:, k, 0:512], k == 0, k == 7, hnTB[0:4] + [slotB], [pB])
                    if kind == 0:
                        S.add("act", lambda e, a=dst[:, 0:512], b=pv[:, 0:512]: e.mul(a, b, 0.0625), [pB], [dstB])
                    else:
                        act(dst[:, 0:512], pv[:, 0:512], AF.Silu, [pB], [dstB])
                    if p == PS_C:
                        pv, pB = bank()
                        for k in range(8):
                            mm(pv[:, 0:64], slot[:, k, cc * 128:(cc + 1) * 128], hnT[:, k, 512:576], k == 0, k == 7, [hnTB[4], slotB], [pB])
                        if kind == 0:
                            S.add("act", lambda e, a=qTs[:, c, :], b=pv[:, 0:64]: e.mul(a, b, 0.0625), [pB], [qTs_B])
                        else:
                            act(mgss[:, c, :], pv[:, 0:64], AF.Silu, [pB], [mgss_B])
        pTt, pTB = wk[6]
        pTv = bf(pTt)
        rst, rsB = wk[7]
        for h in range(4):
            psum_, psumB = bank()
            pin(psumB)
            po2, po2B = bank2()
            pin(po2B)
            for mt in range(2):
                pv, pB = bank()
                for dc in range(2):
                    q_, qB_ = qc(2 * h + dc)
                    mm(pv[:, 0:512], kT[:, 2 * h + dc, mt * 128:(mt + 1) * 128], q_[:, 0:512], dc == 0, dc == 1, [kT_B, qB_], [pB])
                act(pTv[:, mt * 512:(mt + 1) * 512], pv[:, 0:512], AF.Exp, [pB], [pTB])
                mm(psum_[:, 0:512], ones_b[:, :], pTv[:, mt * 512:(mt + 1) * 512], mt == 0, mt == 1, [ones_b_B, pTB], [psumB])
                for dc in range(2):
                    mm(po2[:, dc * 512:(dc + 1) * 512], v_tm[:, mt, (2 * h + dc) * 128:(2 * h + dc + 1) * 128], pTv[:, mt * 512:(mt + 1) * 512],
                       mt == 0, mt == 1, [v_tm_B, pTB], [po2B[dc]])
            S.add("dve", lambda e, a=rst[:, 0:512], b=psum_[:, 0:512]: e.reciprocal(a, b), [psumB], [rsB])
            for dc in range(2):
                g_, gB_ = gc(2 * h + dc)
                tt("dve", rst[:, 512:1024], po2[:, dc * 512:(dc + 1) * 512], rst[:, 0:512], ALU.mult, [po2B[dc], rsB], [rsB])
                tt("dve", mixT[:, 16 + 2 * h + dc, 0:512], rst[:, 512:1024], g_[:, 0:512], ALU.mult, [rsB, gB_], [mxB[16 + 2 * h + dc][0]])
            unpin(psumB)
            unpin(po2B)
        if p == PS_C:
            for sq in range(16):
                s5, s5B = sm[(5, 12)[sq % 2]]
                s6, s6B = sm[(6, 13)[sq % 2]]
                pTs = s5[:].bitcast(BF16)
                Kt, KB = wk[8 + sq % 2]
                Vt, VB = wk[10 + sq % 2]
                Kb = bf(Kt).rearrange("p (m e) -> p m e", m=2)
                Vb = bf(Vt).rearrange("p (m e) -> p m e", m=2)
                S.dma("pool", Kb, ck[sq].rearrange("(m p) e -> p m e", p=128), writes=[KB])
                S.dma("pool", Vb, cv[sq].rearrange("(m p) e -> p m e", p=128), writes=[VB])
                p2i = bank2_bf()
                p2b, p2B = p2i
                for c in range(8):
                    for mt in range(2):
                        tr(p2b[:, c * 256 + mt * 128:c * 256 + mt * 128 + 128], Kb[:, mt, c * 128:(c + 1) * 128], ident_b[:, :],
                           [KB, ident_b_B], [p2B[c // 4]])
                kts_t, ktsB = wk[(12, 13)[sq % 2]]
                kTs = bf(kts_t).rearrange("p (c m) -> p c m", c=8)
                for hb in range(2):
                    cp("dve", kTs[:, hb * 4:(hb + 1) * 4, :], p2b[:, hb * 1024:(hb + 1) * 1024].rearrange("p (c m) -> p c m", c=4),
                       [p2B[hb]], [ktsB])
                pv, pB = bank()
                for h in range(4):
                    for mt in range(2):
                        for dc in range(2):
                            mm(pv[:, (h * 2 + mt) * 4:(h * 2 + mt) * 4 + 4], kTs[:, 2 * h + dc, mt * 128:(mt + 1) * 128],
                               qTs[:, 2 * h + dc, 4 * sq:4 * sq + 4], dc == 0, dc == 1, [ktsB, qTs_B], [pB])
                act(pTs[:, 0:32], pv[:, 0:32], AF.Exp, [pB], [s5B])
                pv2, pB2 = bank()
                for h in range(4):
                    for mt in range(2):
                        mm(pv2[:, h * 4:(h + 1) * 4], ones_b[:, :], pTs[:, (h * 2 + mt) * 4:(h * 2 + mt) * 4 + 4], mt == 0, mt == 1,
                           [ones_b_B, s5B], [pB2])
                for c in range(8):
                    h = c // 2
                    for mt in range(2):
                        mm(pv2[:, 64 + c * 4:64 + c * 4 + 4], Vb[:, mt, c * 128:(c + 1) * 128], pTs[:, (h * 2 + mt) * 4:(h * 2 + mt) * 4 + 4],
                           mt == 0, mt == 1, [VB, s5B], [pB2])
                S.add("dve", lambda e, a=s6[:, 0:16], b=pv2[:, 0:16]: e.reciprocal(a, b), [pB2], [s6B])
                tt("dve", s6[:, 16:48].rearrange("p (h d t) -> p h d t", h=4, d=2), pv2[:, 64:96].rearrange("p (h d t) -> p h d t", h=4, d=2),
                   s6[:, 0:16].rearrange("p (h t) -> p h t", h=4).unsqueeze(2).to_broadcast([128, 4, 2, 4]), ALU.mult, [pB2, s6B], [s6B])
                tt("dve", mixT[:, 16:24, 512 + 4 * sq:512 + 4 * sq + 4], s6[:, 16:48].rearrange("p (c t) -> p c t", c=8),
                   mgss[:, :, 4 * sq:4 * sq + 4], ALU.mult, [s6B, mgss_B], [mxB[k_][1] for k_ in range(16, 24)])

        if stop == "C":
            return nc, S
        mark("D%d" % p)
        fn_bc, fn_bcB = load_bc(fng)
        for i, (c0, L, smp) in enumerate(tilesD):
            src = x_s[:, :] if smp else x_p[p * 512 + c0:p * 512 + c0 + 128, :]
            S.dma("sp", wk[i][0][:L, :], src, writes=[wk[i][1]])
        for half in range(2):
            slots = [load_w(w_out[s_ * 1024:(s_ + 1) * 1024, half * 512:(half + 1) * 512], 8, 512, key=("out", s_, half)) for s_ in range(3)]
            for i, (c0, L, smp) in enumerate(tilesD):
                yt, ytB = wk[i]
                pv, pB = bank()
                for k in range(24):
                    slt, sltB = slots[k // 8]
                    mm(pv[:L, 0:512], mixT[:, k, c0:c0 + L], slt[:, k % 8, :], k == 0, k == 23, [mxB[k][1 if smp else 0], sltB], [pB])
                tt("dve", yt[:L, half * 512:(half + 1) * 512], pv[:L, 0:512], yt[:L, half * 512:(half + 1) * 512], ALU.add, [pB, ytB], [ytB])
        for i, (c0, L, smp) in enumerate(tilesD):
            yt, ytB = wk[i]
            ot, otB = wk[5 + i % 2]
            st, stB = sm[(0, 7)[i % 2]]
            mean_var(yt[:L, :], L, 1024, ytB, st, stB, 0)
            rstd_from(st, stB, L, 1, 2, use_mean_col=0)
            stt("dve", ot[:L, :], yt[:L, :], st[:L, 2:3], fn_bc[:L, :], ALU.mult, ALU.mult, [ytB, stB, fn_bcB], [otB])
            dst = y_s[:, :] if smp else y_p[p * 512 + c0:p * 512 + c0 + 128, :]
            S.dma("sp", dst, ot[:L, :], reads=[otB])
        if stop == "D":
            return nc, S

    return nc, S


def _consts():
    f = np.float32
    c = {}
    c["c_ident"] = np.eye(128, dtype=f)
    i = np.arange(128)
    c["c_utri"] = (i[:, None] <= i[None, :]).astype(f)
    j = np.arange(64)
    same = (j[:, None] // 4 == j[None, :] // 4)
    c["c_same_s"] = same.astype(f)
    c["c_utri_s"] = (same & (j[:, None] <= j[None, :])).astype(f)
    mp = np.where(i[None, :] >= i[:, None], 0.0, NEG).astype(f)
    c["c_mask_p"] = np.tile(mp, (1, 4))
    ms = np.where(same & (j[None, :] >= j[:, None]), 0.0, NEG).astype(f)
    c["c_mask_s"] = np.tile(ms, (1, 4))
    d96 = np.zeros((96, 4, 128), f)
    d96s = np.zeros((96, 4, 64), f)
    for r in range(96):
        if r % 32 < 4:
            d96[r, r % 32, :] = 1.0
            d96s[r, r % 32, :] = 1.0
    c["c_d96"] = d96.reshape(96, 512)
    c["c_d96s"] = d96s.reshape(96, 256)
    sq = (np.arange(16)[:, None] == (j[None, :] // 4)).astype(f)
    c["c_seqm"] = sq.reshape(1, 1024)
    c["c_seqm_t"] = np.ascontiguousarray(sq.T)
    r16 = np.arange(16)
    c["c_oh16"] = (r16[:, None] % 8 == np.arange(8)[None, :]).astype(f)
    c["c_msel"] = np.stack([(r16 < 8), (r16 >= 8)], 1).astype(f)
    c["c_sel4"] = (np.arange(4)[:, None] == (j[None, :] % 4)).astype(f)
    return c


_PROG = None


def kernel(x_prompt, x_sample, mem_prompt, state_ssm, state_conv, cache_mem_k, cache_mem_v,
           norm_g, w_in, gm_norm_g, gm_norm_b, gm_w_spatial, gm_b_spatial, conv_w, conv_b,
           dt_bias, a_log, d_skip, ssm_norm_g, mem_norm_g, w_mem_k, w_mem_v, w_out, final_norm_g):
    global _PROG
    f = np.float32
    A = lambda a: np.ascontiguousarray(np.asarray(a, dtype=f))
    if _PROG is None:
        nc, S = build_program()
        S.emit()
        _PROG = nc
    nc = _PROG
    consts = _consts()
    shared = dict(
        norm_g=A(norm_g).reshape(1, 1024), w_in=A(w_in)[0], gm_norm_g=A(gm_norm_g).reshape(1, 1024),
        gm_norm_b=A(gm_norm_b).reshape(1, 1024), gm_w=A(gm_w_spatial)[0], gm_bs=A(gm_b_spatial)[0],
        conv_w=A(conv_w)[0], conv_b=A(conv_b).reshape(1, 1536), dt_bias=A(dt_bias).reshape(1, 16),
        a_log=A(a_log).reshape(1, 16), d_skip=A(d_skip).reshape(1, 16), ssm_norm_g=A(ssm_norm_g).reshape(1, 1024),
        mem_norm_g=A(mem_norm_g).reshape(1, 1024), w_mk=A(w_mem_k)[0], w_mv=A(w_mem_v)[0], w_out=A(w_out)[0],
        fng=A(final_norm_g).reshape(1, 1024), **consts)
    xp = A(x_prompt); xs = A(x_sample); mp = A(mem_prompt)
    ss = A(state_ssm)[0]; sc = A(state_conv)[0]; ckk = A(cache_mem_k)[0]; cvv = A(cache_mem_v)[0]
    in_maps = []
    for c in range(NCORES):
        b0 = 16 * c
        m = dict(shared)
        m["x_p"] = xp[c]
        m["x_s"] = xs[b0:b0 + 16].reshape(64, 1024)
        m["mem"] = mp[c]
        m["st_ssm"] = ss[b0:b0 + 16].reshape(16, 8, 128, 128)
        m["st_conv"] = sc[b0:b0 + 16].reshape(48, 1536)
        m["ck"] = ckk[b0:b0 + 16].reshape(16, 256, 1024)
        m["cv"] = cvv[b0:b0 + 16].reshape(16, 256, 1024)
        in_maps.append(m)
    res = run_bass_kernel_spmd(nc, in_maps, core_ids=list(range(NCORES)))
    R = res.results
    cat = lambda k: [np.asarray(R[c][k], dtype=f) for c in range(NCORES)]
    y_prompt = np.stack(cat("y_p"), 0)
    y_sample = np.concatenate([a.reshape(16, 4, 1024) for a in cat("y_s")], 0)
    ssm_prompt = np.stack([a.reshape(16, 64, 128) for a in cat("o_ssm_p")], 0)[None]
    conv_prompt = np.stack(cat("o_conv_p"), 0)[None]
    mem_k = np.stack([a.reshape(256, 4, 256) for a in cat("o_mk")], 0)[None]
    mem_v = np.stack([a.reshape(256, 4, 256) for a in cat("o_mv")], 0)[None]
    ssm_sample = np.concatenate([a.reshape(16, 16, 64, 128) for a in cat("o_ssm_s")], 0)[None]
    conv_sample = np.concatenate([a.reshape(16, 3, 1536) for a in cat("o_conv_s")], 0)[None]
    gv = np.concatenate([a.reshape(16, 4, 1024) for a in cat("o_gv")], 0)[None]
    return (y_prompt, y_sample, ssm_prompt, conv_prompt, mem_k, mem_v, ssm_sample, conv_sample, gv)
```
